# Optimizing a Trainium2 kernel written in Bass

```python
import jax, jax.numpy as jnp
from jax import lax
import numpy as np

D_MODEL = 1024
BATCH = 8
SEQ = 2048
DEPTH = 4
DEC_BATCH = 128
DEC_SEQ = 8
PAST_LEN = 16384
PAGE_SIZE = 128

D_MIX = 2 * D_MODEL
W_POOL = D_MIX // 2
W_LRU = D_MIX - W_POOL
POOL_WINDOWS = (2, 4, 8, 16)
N_POOL_GROUPS = 4
POOL_GROUP = W_POOL // N_POOL_GROUPS
POOL_BUF = 15
LRU_HEADS = 8
LRU_HEAD_DIM = W_LRU // LRU_HEADS
CONV_WIDTH = 4
CONV_BUF = CONV_WIDTH - 1
LRU_C = 8.0
NORM_EPS = 1e-6
D_IN = 2 * W_POOL + 2 * W_LRU

kernel_name = "hymba_pool_rglru_decoder_step"


def rms_norm(x, g):
    xf = x.astype(jnp.float32)
    y = xf * lax.rsqrt(jnp.mean(xf * xf, axis=-1, keepdims=True) + NORM_EPS)
    return (y * g.astype(jnp.float32)).astype(x.dtype)


def pool_mixer(v, buf, start_pos, pool_w, pool_scale):
    B, T = v.shape[0], v.shape[1]
    ext = jnp.concatenate([buf.astype(jnp.float32), v.astype(jnp.float32)], axis=1)
    cs0 = jnp.concatenate([jnp.zeros((B, 1, W_POOL), jnp.float32), jnp.cumsum(ext, axis=1)], axis=1)
    pos = start_pos + jnp.arange(T, dtype=jnp.int32)
    xcur = ext[:, POOL_BUF:, :]
    outs = []
    for g in range(N_POOL_GROUPS):
        w = POOL_WINDOWS[g]
        c0, c1 = g * POOL_GROUP, (g + 1) * POOL_GROUP
        hi = cs0[:, POOL_BUF + 1:POOL_BUF + 1 + T, c0:c1]
        lo = cs0[:, POOL_BUF + 1 - w:POOL_BUF + 1 - w + T, c0:c1]
        cnt = jnp.minimum(pos + 1, w).astype(jnp.float32)[None, :, None]
        outs.append((hi - lo) / cnt - xcur[:, :, c0:c1])
    d = jnp.stack(outs, axis=2)
    y = jnp.einsum('btgc,gcd->btgd', d, pool_w.astype(jnp.float32)).reshape(B, T, W_POOL)
    y = y * pool_scale.astype(jnp.float32)[None, None, :]
    new_buf = ext[:, T:, :]
    return y.astype(v.dtype), new_buf.astype(buf.dtype)


def causal_conv(u, buf, conv_w, conv_b):
    T = u.shape[1]
    ext = jnp.concatenate([buf.astype(jnp.float32), u.astype(jnp.float32)], axis=1)
    cw = conv_w.astype(jnp.float32)
    y = conv_b.astype(jnp.float32)[None, None, :]
    for k in range(CONV_WIDTH):
        y = y + ext[:, k:k + T, :] * cw[k][None, None, :]
    return y.astype(u.dtype), ext[:, T:, :].astype(buf.dtype)


def rg_lru(xs, h0, start_pos, wa, ba, wx, bx, lam):
    B, T = xs.shape[0], xs.shape[1]
    xf = xs.astype(jnp.float32)
    xh = xf.reshape(B, T, LRU_HEADS, LRU_HEAD_DIM)
    r = jax.nn.sigmoid(jnp.einsum('bthi,hij->bthj', xh, wa.astype(jnp.float32)).reshape(B, T, W_LRU)
                       + ba.astype(jnp.float32)[None, None, :])
    gi = jax.nn.sigmoid(jnp.einsum('bthi,hij->bthj', xh, wx.astype(jnp.float32)).reshape(B, T, W_LRU)
                        + bx.astype(jnp.float32)[None, None, :])
    log_a = -LRU_C * r * jax.nn.softplus(-lam.astype(jnp.float32))[None, None, :]
    a = jnp.exp(log_a)
    mult = jnp.sqrt(jnp.maximum(-jnp.expm1(2.0 * log_a), 1e-12))
    pos = start_pos + jnp.arange(T, dtype=jnp.int32)
    is_start = (pos == 0).astype(jnp.float32)[None, :, None]
    mult = is_start + (1.0 - is_start) * mult
    u = xf * gi * mult

    def step(h, au):
        a_t, u_t = au
        h_new = a_t * h + u_t
        return h_new, h_new

    hT, hs = lax.scan(step, h0.astype(jnp.float32), (jnp.swapaxes(a, 0, 1), jnp.swapaxes(u, 0, 1)))
    return jnp.swapaxes(hs, 0, 1).astype(xs.dtype), hT.astype(h0.dtype)


def layer(x, pool_buf, conv_buf, h0, start_pos, norm_pre, norm_post, w_in, pool_w, pool_scale,
          conv_w, conv_b, lru_wa, lru_ba, lru_wx, lru_bx, lru_lam, w_out):
    xn = rms_norm(x, norm_pre)
    z = jnp.einsum('btd,de->bte', xn, w_in.astype(xn.dtype))
    v_pool = z[:, :, 0:W_POOL]
    g_pool = z[:, :, W_POOL:2 * W_POOL]
    u_lru = z[:, :, 2 * W_POOL:2 * W_POOL + W_LRU]
    g_lru = z[:, :, 2 * W_POOL + W_LRU:D_IN]
    y_pool, new_pool = pool_mixer(v_pool, pool_buf, start_pos, pool_w, pool_scale)
    xs, new_conv = causal_conv(u_lru, conv_buf, conv_w, conv_b)
    y_lru, new_h = rg_lru(xs, h0, start_pos, lru_wa, lru_ba, lru_wx, lru_bx, lru_lam)
    mixed = jnp.concatenate([y_pool * jax.nn.silu(g_pool), y_lru * jax.nn.silu(g_lru)], axis=-1)
    out = jnp.einsum('bte,ed->btd', mixed, w_out.astype(mixed.dtype))
    return x + rms_norm(out, norm_post), new_pool, new_conv, new_h


def setup_inputs(seed: int = 0) -> dict:
    key = jax.random.key(seed)
    ks = jax.random.split(key, 20)
    f32 = jnp.float32
    a0 = jax.random.uniform(ks[15], (DEPTH, W_LRU), f32, 0.9, 0.999)
    a_base = a0 ** (1.0 / LRU_C)
    return {
        "x_prompt": jax.random.normal(ks[0], (BATCH, SEQ, D_MODEL), f32),
        "x_sample": jax.random.normal(ks[1], (DEC_BATCH, DEC_SEQ, D_MODEL), f32),
        "state_pool": jax.random.normal(ks[2], (DEPTH, DEC_BATCH, POOL_BUF, W_POOL), f32),
        "state_conv": jax.random.normal(ks[3], (DEPTH, DEC_BATCH, CONV_BUF, W_LRU), f32),
        "state_lru": 0.5 * jax.random.normal(ks[4], (DEPTH, DEC_BATCH, W_LRU), f32),
        "norm_pre": 1.0 + 0.1 * jax.random.normal(ks[5], (DEPTH, D_MODEL), f32),
        "norm_post": 1.0 + 0.1 * jax.random.normal(ks[6], (DEPTH, D_MODEL), f32),
        "w_in": jax.random.normal(ks[7], (DEPTH, D_MODEL, D_IN), f32) * D_MODEL ** -0.5,
        "pool_w": jax.random.normal(ks[8], (DEPTH, N_POOL_GROUPS, POOL_GROUP, POOL_GROUP), f32) * POOL_GROUP ** -0.5,
        "pool_scale": 1.0 + 0.1 * jax.random.normal(ks[9], (DEPTH, W_POOL), f32),
        "conv_w": jax.random.normal(ks[10], (DEPTH, CONV_WIDTH, W_LRU), f32) * CONV_WIDTH ** -0.5,
        "conv_b": 0.02 * jax.random.normal(ks[11], (DEPTH, W_LRU), f32),
        "lru_wa": jax.random.normal(ks[12], (DEPTH, LRU_HEADS, LRU_HEAD_DIM, LRU_HEAD_DIM), f32) * LRU_HEAD_DIM ** -0.5,
        "lru_ba": 0.02 * jax.random.normal(ks[13], (DEPTH, W_LRU), f32),
        "lru_wx": jax.random.normal(ks[14], (DEPTH, LRU_HEADS, LRU_HEAD_DIM, LRU_HEAD_DIM), f32) * LRU_HEAD_DIM ** -0.5,
        "lru_bx": 0.02 * jax.random.normal(ks[16], (DEPTH, W_LRU), f32),
        "lru_lam": jnp.log(a_base) - jnp.log1p(-a_base),
        "w_out": jax.random.normal(ks[17], (DEPTH, D_MIX, D_MODEL), f32) * D_MIX ** -0.5,
    }


def reference(x_prompt, x_sample, state_pool, state_conv, state_lru, norm_pre, norm_post, w_in,
              pool_w, pool_scale, conv_w, conv_b, lru_wa, lru_ba, lru_wx, lru_bx, lru_lam, w_out):
    n_prompt = x_prompt.shape[0]
    xp = x_prompt
    xq = x_sample
    pp, cp, hp, psm, csm, hsm = [], [], [], [], [], []
    for l in range(DEPTH):
        params = (norm_pre[l], norm_post[l], w_in[l], pool_w[l], pool_scale[l], conv_w[l], conv_b[l],
                  lru_wa[l], lru_ba[l], lru_wx[l], lru_bx[l], lru_lam[l], w_out[l])
        zero_pool = jnp.zeros((n_prompt, POOL_BUF, W_POOL), state_pool.dtype)
        zero_conv = jnp.zeros((n_prompt, CONV_BUF, W_LRU), state_conv.dtype)
        zero_h = jnp.zeros((n_prompt, W_LRU), state_lru.dtype)
        xp, pool_p, conv_p, h_p = layer(xp, zero_pool, zero_conv, zero_h, 0, *params)
        pp.append(pool_p)
        cp.append(conv_p)
        hp.append(h_p)
        xq, pool_s, conv_s, h_s = layer(xq, state_pool[l], state_conv[l], state_lru[l], PAST_LEN, *params)
        psm.append(pool_s)
        csm.append(conv_s)
        hsm.append(h_s)
    new_pool_prompt = jnp.stack(pp, axis=0)
    new_conv_prompt = jnp.stack(cp, axis=0)
    new_lru_prompt = jnp.stack(hp, axis=0)
    new_pool_sample = jnp.stack(psm, axis=0)
    new_conv_sample = jnp.stack(csm, axis=0)
    new_lru_sample = jnp.stack(hsm, axis=0)
    return (xp, xq, new_pool_prompt, new_conv_prompt, new_lru_prompt, new_pool_sample, new_conv_sample, new_lru_sample)
```

```python
import os
import numpy as np
import concourse.bass as bass
import concourse.mybir as mybir
from concourse.bass_utils import run_bass_kernel_spmd

F32 = mybir.dt.float32
BF16 = mybir.dt.bfloat16
AF = mybir.ActivationFunctionType
ALU = mybir.AluOpType

NCORES = 8
D = 1024
DEPTH = 4
SEQ = 2048
NS = 16
TS = 8
NTOK = SEQ + NS * TS
KD = D // 128
WINS = (2, 4, 8, 16)
EPS = 1e-6
HP = 15
HC = 3
BUFW = 528

V_NPRE, V_NPOST, V_PSC, V_CB, V_BA, V_BX, V_LAM, V_CW = 0, 32, 64, 96, 128, 160, 192, 224
NVEC = 352


class Res:
    __slots__ = ("w", "r")

    def __init__(self):
        self.w = None
        self.r = {}


class Tracker:
    def __init__(self, nc, n_dma_sems=12):
        self.nc = nc
        self.eng = {}
        self.sems = {}
        self.snap = {}
        self.n_dma_sems = n_dma_sems
        self.dpool = {}
        self.dnext = {}
        self.dsem = []
        self.dcount = []
        self.nwaits = 0
        self.nops = 0

    def add_engine(self, name, handle):
        sem = self.nc.alloc_semaphore("e_" + name)
        self.eng[name] = dict(h=handle, sem=sem, count=0, seen={})
        self.sems[name] = sem

    def add_dma_queue(self, name):
        idxs = []
        for i in range(self.n_dma_sems):
            j = len(self.dsem)
            self.dsem.append(self.nc.alloc_semaphore("dq_%s%d" % (name, i)))
            self.dcount.append(0)
            self.sems[("d", j)] = self.dsem[j]
            idxs.append(j)
        self.dpool[name] = idxs
        self.dnext[name] = 0

    def _wait(self, E, key, n):
        if E["seen"].get(key, 0) >= n:
            return
        E["h"].wait_ge(self.sems[key], n)
        self.nwaits += 1
        E["seen"][key] = n
        sn = self.snap.get((key, n))
        if sn:
            for k2, n2 in sn.items():
                if E["seen"].get(k2, 0) < n2:
                    E["seen"][k2] = n2

    def _deps(self, ename, E, reads, writes):
        for r in reads:
            if r.w is not None:
                self._wait(E, r.w[0], r.w[1])
        for w in writes:
            if w.w is not None:
                self._wait(E, w.w[0], w.w[1])
            for k, n in w.r.items():
                self._wait(E, k, n)

    def _commit(self, key, n, reads, writes):
        for r in reads:
            if r.r.get(key, 0) < n:
                r.r[key] = n
        for w in writes:
            w.w = (key, n)
            w.r = {}

    def op(self, ename, emit, reads=(), writes=()):
        E = self.eng[ename]
        self._deps(ename, E, reads, writes)
        ins = emit(E["h"])
        ins.then_inc(E["sem"], 1)
        E["count"] += 1
        n = E["count"]
        sn = {k: v for k, v in E["seen"].items() if not isinstance(k, tuple)}
        sn[ename] = n
        self.snap[(ename, n)] = sn
        self._commit(ename, n, reads, writes)
        self.nops += 1

    def dma(self, ename, out, in_, reads=(), writes=()):
        E = self.eng[ename]
        pool = self.dpool[ename]
        s = pool[self.dnext[ename]]
        self.dnext[ename] = (self.dnext[ename] + 1) % len(pool)
        key = ("d", s)
        if self.dcount[s] > 0:
            self._wait(E, key, self.dcount[s])
        self._deps(ename, E, reads, writes)
        E["h"].dma_start(out=out, in_=in_).then_inc(self.dsem[s], 16)
        self.dcount[s] += 16
        n = self.dcount[s]
        self.snap[(key, n)] = {k: v for k, v in E["seen"].items() if not isinstance(k, tuple)}
        self._commit(key, n, reads, writes)

    def finish(self, ename):
        E = self.eng[ename]
        for s in range(len(self.dsem)):
            if self.dcount[s] > 0:
                self._wait(E, ("d", s), self.dcount[s])
        for k, e2 in self.eng.items():
            if k != ename and e2["count"] > 0:
                self._wait(E, k, e2["count"])


def build_program(n_layers=DEPTH):
    nc = bass.Bass("TRN2", target_bir_lowering=False)
    T = Tracker(nc)
    T.add_engine("pe", nc.tensor)
    T.add_engine("act", nc.scalar)
    T.add_engine("dve", nc.vector)
    T.add_engine("pool", nc.gpsimd)
    T.add_engine("sp", nc.sync)
    T.add_dma_queue("sp")
    T.add_dma_queue("pool")

    def din(name, shape):
        return nc.dram_tensor(name, shape, F32, kind="ExternalInput").ap()

    def dout(name, shape):
        return nc.dram_tensor(name, shape, F32, kind="ExternalOutput").ap()

    xp_d = din("xp", [SEQ, D])
    xs_d = din("xs", [NS * TS, D])
    spool_d = din("spool", [DEPTH, NS * HP, D])
    sconv_d = din("sconv", [DEPTH, NS * HC, D])
    slru_d = din("slru", [DEPTH, NS, D])
    vecs_d = din("vecs", [NVEC, 128])
    ident_d = din("ident", [128, 128])
    w_in_d = din("w_in_r", [DEPTH, 8, 128, KD * 512])
    pool_w_d = din("pool_w", [DEPTH, 4, 256, 256])
    lru_wa_d = din("lru_wa", [DEPTH, 8, 128, 128])
    lru_wx_d = din("lru_wx", [DEPTH, 8, 128, 128])
    w_out_d = din("w_out_r", [DEPTH, KD, 128, 16 * 128])

    yp_d = dout("yp", [SEQ, D])
    ys_d = dout("ys", [NS * TS, D])
    npp_d = dout("npp", [DEPTH, HP, D])
    ncp_d = dout("ncp", [DEPTH, HC, D])
    nlp_d = dout("nlp", [DEPTH, 1, D])
    nps_d = dout("nps", [DEPTH, NS * HP, D])
    ncs_d = dout("ncs", [DEPTH, NS * HC, D])
    nls_d = dout("nls", [DEPTH, NS, D])

    def sb(name, shape, dt=F32):
        return nc.alloc_sbuf_tensor(name, shape, dt).ap()

    TILES = [dict(kind="p", t0=i * 512, n=512, nt=512, first=(i == 0), last=(i == 3)) for i in range(4)]
    TILES.append(dict(kind="s", t0=SEQ, n=NS * TS, nt=TS, first=False, last=False))
    for i, t in enumerate(TILES):
        t["i"] = i
    NT = len(TILES)

    NWS = 3
    X = sb("X", [128, KD, NTOK])
    XN = [sb("XN%d" % i, [128, KD, 512], BF16) for i in range(2)]
    MIX = sb("MIX", [128, 16, 512], BF16)
    NWO = 4
    WOP = [sb("WOP%d" % i, [128, 16, 128], BF16) for i in range(NWO)]
    WIN = [sb("WIN%d" % i, [128, KD, 512], BF16) for i in range(NWS)]
    SW = [sb("SW%d" % i, [128, 512], BF16) for i in range(NWS)]
    VT = sb("VT", [128, NVEC])
    DER = sb("DER", [128, 5, 32])
    IDENT = sb("IDENT", [128, 128])
    ONES = sb("ONES", [128, 128], BF16)
    RC = sb("RC", [128, 4, 16])
    HS = sb("HS", [128, KD, 19])
    SSP = sb("SSP", [128, KD, NS, HP])
    SSC = sb("SSC", [128, KD, NS, HC])
    SSH = sb("SSH", [128, KD, NS])
    IO = sb("IO", [128, 512])
    RSTD1 = sb("RSTD1", [128, 512])
    RSTD3 = RSTD1
    NSQ = 4
    SQ = [sb("SQ%d" % i, [128, 512], BF16) for i in range(NSQ)]
    TMP16 = sb("TMP16", [128, NS])
    NCAR = 4
    CAR = [dict(N0=sb("cN0_%d" % i, [128, 512]), N1=sb("cN1_%d" % i, [128, 512]),
                D0=sb("cD0_%d" % i, [128, 512], BF16)) for i in range(NCAR)]
    PD1 = sb("PD1", [128, 512], BF16)
    rPD1 = Res()
    HB = [sb("HB%d" % i, [128, BUFW]) for i in range(2)]
    HPB = [sb("HPB%d" % i, [128, BUFW]) for i in range(3)]
    HD = [[sb("HD%d_%d" % (i, j), [128, 512]) for j in range(3)] for i in range(2)]
    PS = [nc.alloc_psum_tensor("ps%d" % i, [128, 512], F32).ap() for i in range(8)]

    rX = [[Res() for _ in range(KD)] for _ in range(NT)]
    rXN = [[Res() for _ in range(KD)] for _ in range(2)]
    rMIX = [Res() for _ in range(16)]
    rWOP = [Res() for _ in range(NWO)]
    rWIN = [[Res(), Res()] for _ in range(NWS)]
    rSW = [[Res(), Res()] for _ in range(NWS)]
    rVT, rDER, rIDENT, rONES, rRC, rTMP16 = Res(), Res(), Res(), Res(), Res(), Res()
    rHS = [Res() for _ in range(KD)]
    rSSP = [Res() for _ in range(KD)]
    rSSC = [Res() for _ in range(KD)]
    rSSH = [Res() for _ in range(KD)]
    rIO = Res()
    rRSTD1 = Res()
    rRSTD3 = rRSTD1
    rSQ = [Res() for _ in range(NSQ)]
    rCAR = [dict(N0=Res(), N1=Res(), D0=Res()) for _ in range(NCAR)]
    rHB = [Res() for _ in range(2)]
    rHPB = [Res() for _ in range(3)]
    rHD = [[Res() for _ in range(3)] for _ in range(2)]
    rPS = [Res() for _ in range(8)]
    NOB = 6
    OBv = [HD[dc % 2][dc // 2] for dc in range(NOB)]
    rOB = [rHD[dc % 2][dc // 2] for dc in range(NOB)]
    ps_next = [0]

    ps_live = [False] * 8

    def ps_alloc():
        for _ in range(8):
            i = ps_next[0]
            ps_next[0] = (i + 1) % 8
            if not ps_live[i]:
                ps_live[i] = True
                return PS[i], rPS[i]
        raise RuntimeError("PSUM banks exhausted")

    def rel(rps):
        ps_live[rPS.index(rps)] = False

    NWARM = int(os.environ.get("MK_NWARM", "4"))

    def warm(n, xb):
        if NWARM <= 0:
            return
        ps, rps = ps_alloc()

        def emit(e):
            ins = None
            for k in range(NWARM):
                ins = e.matmul(ps[:, 0:n], lhsT=ONES, rhs=XN[xb][:, k % KD, 0:n], start=(k == 0), stop=(k == NWARM - 1))
            return ins
        T.op("pe", emit, reads=[rONES] + rXN[xb], writes=[rps])
        rel(rps)

    io_pool = [(IO, rIO)]
    io_next = [0]

    def io_alloc():
        i = io_next[0] % len(io_pool)
        io_next[0] += 1
        return io_pool[i]

    def col(base, l, k):
        c = base + l * 8 + k
        return VT[:, c:c + 1]

    def dcol(j, l, k):
        return DER[:, j, l * 8 + k:l * 8 + k + 1]

    def as_t(ap2d, tl):
        if tl["kind"] == "p":
            return ap2d
        return ap2d.rearrange("p (s r) -> p s r", s=NS)

    def tview(buf, tl, H):
        if tl["kind"] == "p":
            return buf[:, 0:H + 512]
        return buf[:, 0:NS * (H + TS)].rearrange("p (s r) -> p s r", s=NS)

    def tsl(view, tl, lo, hi):
        if tl["kind"] == "p":
            return view[:, lo:hi]
        return view[:, :, lo:hi]

    for _i in range(2):
        for _j in range(3):
            io_pool.append((HD[_i][_j], rHD[_i][_j]))
    T.dma("sp", IDENT, ident_d, writes=[rIDENT])
    T.op("pool", lambda e: e.memset(ONES, 1.0), writes=[rONES])
    T.op("pool", lambda e: e.memset(RC, 1.0), writes=[rRC])
    for g, w in enumerate(WINS):
        for t in range(w - 1):
            T.op("pool", lambda e, g=g, t=t: e.memset(RC[:, g, t:t + 1], 1.0 / (t + 1)), writes=[rRC])
    r0 = 0
    while r0 < NVEC:
        nr = min(128, NVEC - r0)
        io, rio = io_alloc()
        T.dma("sp", io[0:nr, 0:128], vecs_d[r0:r0 + nr, :], writes=[rio])
        ps, rps = ps_alloc()
        T.op("pe", lambda e, io=io, nr=nr, ps=ps: e.transpose(ps[:, 0:nr], io[0:nr, 0:128], IDENT[0:nr, 0:nr]),
             reads=[rio, rIDENT], writes=[rps])
        T.op("act", lambda e, ps=ps, r0=r0, nr=nr: e.copy(VT[:, r0:r0 + nr], ps[:, 0:nr]), reads=[rps], writes=[rVT])
        rel(rps)
        r0 += nr
    T.op("dve", lambda e: e.tensor_scalar(DER[:, 0, :], VT[:, V_BA:V_BA + 32], 0.5, None, op0=ALU.mult), reads=[rVT], writes=[rDER])
    T.op("dve", lambda e: e.tensor_scalar(DER[:, 1, :], VT[:, V_BX:V_BX + 32], 0.5, None, op0=ALU.mult), reads=[rVT], writes=[rDER])
    T.op("act", lambda e: e.activation(DER[:, 4, :], VT[:, V_LAM:V_LAM + 32], AF.Exp, scale=-1.0), reads=[rVT], writes=[rDER])
    T.op("act", lambda e: e.activation(DER[:, 4, :], DER[:, 4, :], AF.Ln, bias=1.0, scale=1.0), reads=[rDER], writes=[rDER])
    T.op("dve", lambda e: e.tensor_scalar(DER[:, 2, :], DER[:, 4, :], -8.0, None, op0=ALU.mult), reads=[rDER], writes=[rDER])
    T.op("dve", lambda e: e.tensor_scalar(DER[:, 3, :], DER[:, 4, :], -4.0, None, op0=ALU.mult), reads=[rDER], writes=[rDER])

    bg = []
    bg_on = [False]

    def tr_in(src, nr, dst_fn, rdst):
        for hh in range(2):
            cell = {}

            def t_dma(hh=hh, cell=cell):
                io, rio = io_alloc()
                cell["io"] = (io, rio)
                T.dma("sp", io[0:nr, :], src[:, hh * 512:(hh + 1) * 512], writes=[rio])

            def t_tr(hh=hh, cell=cell):
                io, rio = cell["io"]
                ps, rps = ps_alloc()

                def emit(e):
                    ins = None
                    for j in range(4):
                        ins = e.transpose(ps[:, j * nr:(j + 1) * nr], io[0:nr, j * 128:(j + 1) * 128], IDENT[0:nr, 0:nr])
                    return ins
                T.op("pe", emit, reads=[rio, rIDENT], writes=[rps])
                T.op("act", lambda e: e.copy(dst_fn(hh), ps[:, 0:4 * nr].rearrange("p (j c) -> p j c", j=4)),
                     reads=[rps], writes=[rdst[hh * 4 + j] for j in range(4)])
                rel(rps)
            if bg_on[0]:
                bg.append(t_dma)
                bg.append(t_tr)
            else:
                t_dma()
                t_tr()

    def load_tokens(src_rows, tl_idx, c0):
        tr_in(src_rows, 128, lambda hh: X[:, hh * 4:hh * 4 + 4, c0:c0 + 128], rX[tl_idx])

    for tl in TILES:
        if tl["kind"] == "p":
            for b in range(4):
                load_tokens(xp_d[tl["t0"] + b * 128:tl["t0"] + (b + 1) * 128, :], tl["i"], tl["t0"] + b * 128)
        else:
            load_tokens(xs_d, tl["i"], tl["t0"])

    def load_sample_state(l):
        for hf in range(2):
            nr = 8 * HP
            tr_in(spool_d[l, hf * nr:(hf + 1) * nr, :], nr,
                  lambda hh, hf=hf: SSP[:, hh * 4:hh * 4 + 4, hf * 8:(hf + 1) * 8, :].rearrange("p k s r -> p k (s r)"), rSSP)
        tr_in(sconv_d[l, :, :], NS * HC, lambda hh: SSC[:, hh * 4:hh * 4 + 4, :, :].rearrange("p k s r -> p k (s r)"), rSSC)
        tr_in(slru_d[l, :, :], NS, lambda hh: SSH[:, hh * 4:hh * 4 + 4, :], rSSH)

    def store_rows(src_view_fn, nr, rsrc, dsts):
        for hh in range(2):
            def t_store(hh=hh):
                io, rio = io_alloc()
                ps, rps = ps_alloc()

                def emit(e):
                    ins = None
                    for j in range(4):
                        k = hh * 4 + j
                        ins = e.transpose(ps[0:nr, j * 128:(j + 1) * 128], src_view_fn(k), IDENT)
                    return ins
                T.op("pe", emit, reads=[rsrc[hh * 4 + j] for j in range(4)] + [rIDENT], writes=[rps])
                T.op("act", lambda e: e.copy(io[0:nr, :], ps[0:nr, :]), reads=[rps], writes=[rio])
                rel(rps)
                for (a, b, dst) in dsts:
                    T.dma("sp", dst[:, hh * 512:(hh + 1) * 512], io[a:b, :], reads=[rio])
            if bg_on[0]:
                bg.append(t_store)
            else:
                t_store()

    def bg_step(nmax=1):
        for _ in range(nmax):
            if bg:
                bg.pop(0)()

    def ones_mm(ps, rps, k, n, sq, rsq):
        T.op("pe", lambda e: e.matmul(ps[:, 0:n], lhsT=ONES, rhs=sq[:, 0:n], start=(k == 0), stop=(k == KD - 1)),
             reads=[rsq, rONES], writes=([rps] if k in (0, KD - 1) else []))

    def rstd_from(ps, rps, n, RS, rRS):
        T.op("act", lambda e: e.activation(RS[:, 0:n], ps[:, 0:n], AF.Ln, bias=EPS, scale=1.0 / D), reads=[rps], writes=[rRS])
        rel(rps)
        T.op("act", lambda e: e.activation(RS[:, 0:n], RS[:, 0:n], AF.Exp, scale=-0.5), reads=[rRS], writes=[rRS])

    sq_next = [0]

    def sq_alloc():
        i = sq_next[0]
        sq_next[0] = (i + 1) % NSQ
        return SQ[i], rSQ[i]

    p1 = {}

    def phase1_sq(l, tl, ks):
        n, t0, ti = tl["n"], tl["t0"], tl["i"]
        for k in ks:
            sq, rsq = sq_alloc()
            T.op("act", lambda e, k=k, sq=sq: e.activation(sq[:, 0:n], X[:, k, t0:t0 + n], AF.Square), reads=[rX[ti][k]], writes=[rsq])
            p1[k] = (sq, rsq)

    def phase1_mm(l, tl, ks):
        n = tl["n"]
        if 0 in ks:
            p1["ps"] = ps_alloc()
        ps, rps = p1["ps"]
        for k in ks:
            sq, rsq = p1[k]
            ones_mm(ps, rps, k, n, sq, rsq)

    def phase1_fin(l, tl, xb):
        n, t0, ti = tl["n"], tl["t0"], tl["i"]
        ps, rps = p1["ps"]
        rstd_from(ps, rps, n, RSTD1, rRSTD1)
        for k in range(KD):
            T.op("dve", lambda e, k=k: e.scalar_tensor_tensor(XN[xb][:, k, 0:n], X[:, k, t0:t0 + n], col(V_NPRE, l, k),
                                                              RSTD1[:, 0:n], op0=ALU.mult, op1=ALU.mult),
                 reads=[rX[ti][k], rRSTD1, rVT], writes=[rXN[xb][k]])

    def phase1(l, tl, xb):
        phase1_sq(l, tl, range(0, 4))
        phase1_mm(l, tl, range(0, 4))
        phase1_sq(l, tl, range(4, 8))
        phase1_mm(l, tl, range(4, 8))
        phase1_fin(l, tl, xb)

    def zmm(wbuf, rw, c, n, xb):
        ps, rps = ps_alloc()

        def emit(e, ps=ps):
            ins = None
            for k in range(KD):
                ins = e.matmul(ps[:, 0:n], lhsT=wbuf[:, k, c * 128:(c + 1) * 128], rhs=XN[xb][:, k, 0:n],
                               start=(k == 0), stop=(k == KD - 1))
            return ins
        T.op("pe", emit, reads=[rw[c // 2]] + rXN[xb], writes=[rps])
        return ps, rps

    def pool_item(l, tl, g, si, ws, xb):
        n, nt = tl["n"], tl["nt"]
        prompt = tl["kind"] == "p"
        w = WINS[g]
        m = w.bit_length() - 1
        car, rcar = CAR[si], rCAR[si]
        B = dict(H0=HPB[0], H1=HPB[1], H2=HPB[2], N0=car["N0"], N1=car["N1"])
        rB = dict(H0=rHPB[0], H1=rHPB[1], H2=rHPB[2], N0=rcar["N0"], N1=rcar["N1"])
        dbufs, rdb = [car["D0"], PD1], [rcar["D0"], rPD1]
        wbuf, rw, sw, rsw = WIN[ws], rWIN[ws], SW[ws], rSW[ws]
        st = {}

        def A():
            st["z"] = [zmm(wbuf, rw, c, n, xb) for c in range(4)]

        def chain(c):
            pc = g * 2 + c
            ps, rps = st["z"][c]
            AV = tview(B["H0"], tl, HP)
            if prompt:
                T.op("act", lambda e: e.copy(AV[:, 0:HP], HS[:, pc, 0:HP]), reads=[rHS[pc]], writes=[rB["H0"]])
            else:
                T.op("act", lambda e: e.copy(AV[:, :, 0:HP], SSP[:, pc, :, :]), reads=[rSSP[pc]], writes=[rB["H0"]])
            T.op("act", lambda e: e.copy(tsl(AV, tl, HP, HP + nt), as_t(ps[:, 0:n], tl)), reads=[rps], writes=[rB["H0"]])
            rel(rps)
            if prompt:
                T.op("pool", lambda e: e.tensor_copy(HS[:, pc, 0:HP], AV[:, nt:nt + HP]), reads=[rB["H0"]], writes=[rHS[pc]])
            else:
                T.op("pool", lambda e: e.tensor_copy(SSP[:, pc, :, :], AV[:, :, nt:nt + HP]), reads=[rB["H0"]], writes=[rSSP[pc]])
            lo = [0] * (m + 1)
            lo[m] = HP
            for j in range(m, 0, -1):
                lo[j - 1] = lo[j] - (1 << (j - 1))
            src_n = "H0"
            names = ["H1", "H2"]
            for j in range(1, m + 1):
                dst_n = names[(j - 1) % 2]
                sh = 1 << (j - 1)
                SV = tview(B[src_n], tl, HP)
                DV = tview(B[dst_n], tl, HP)
                T.op("dve", lambda e, SV=SV, DV=DV, j=j, sh=sh: e.tensor_tensor(
                    tsl(DV, tl, lo[j], HP + nt), tsl(SV, tl, lo[j], HP + nt), tsl(SV, tl, lo[j] - sh, HP + nt - sh), ALU.add),
                    reads=[rB[src_n]], writes=[rB[dst_n]])
                src_n = dst_n
            SV = tview(B[src_n], tl, HP)
            T.op("dve", lambda e: e.scalar_tensor_tensor(
                as_t(dbufs[c][:, 0:n], tl), tsl(SV, tl, HP, HP + nt), 1.0 / w, tsl(AV, tl, HP, HP + nt), op0=ALU.mult, op1=ALU.subtract),
                reads=[rB[src_n], rB["H0"]], writes=[rdb[c]])
            if tl["first"]:
                tmp_n = names[m % 2]
                TV = B[tmp_n]
                T.op("dve", lambda e: e.tensor_tensor(TV[:, 0:w - 1], SV[:, HP:HP + w - 1], RC[:, g, 0:w - 1], ALU.mult),
                     reads=[rB[src_n], rRC], writes=[rB[tmp_n]])
                T.op("dve", lambda e: e.tensor_tensor(dbufs[c][:, 0:w - 1], TV[:, 0:w - 1], AV[:, HP:HP + w - 1], ALU.subtract),
                     reads=[rB[tmp_n], rB["H0"]], writes=[rdb[c]])

        def Bst():
            chain(0)
            for mo in range(2):
                psg, rpsg = st["z"][2 + mo]
                sg_n = ("N0", "N1")[mo]
                T.op("act", lambda e, psg=psg, sg_n=sg_n: e.activation(B[sg_n][:, 0:n], psg[:, 0:n], AF.Silu), reads=[rpsg], writes=[rB[sg_n]])
                rel(rpsg)
            chain(1)

        def C():
            warm(n, xb)
            st["y"] = []
            for mo in range(2):
                psy, rpsy = ps_alloc()

                def emit(e, psy=psy, mo=mo):
                    ins = None
                    for k in range(2):
                        ins = e.matmul(psy[:, 0:n], lhsT=sw[:, k * 256 + mo * 128:k * 256 + (mo + 1) * 128], rhs=dbufs[k][:, 0:n],
                                       start=(k == 0), stop=(k == 1))
                    return ins
                T.op("pe", emit, reads=rsw + [rdb[0], rdb[1]], writes=[rpsy])
                st["y"].append((psy, rpsy))

        def Dst():
            for mo in range(2):
                pc = g * 2 + mo
                psy, rpsy = st["y"][mo]
                sg_n = ("N0", "N1")[mo]
                T.op("dve", lambda e, psy=psy, sg_n=sg_n, pc=pc: e.scalar_tensor_tensor(
                    MIX[:, pc, 0:n], psy[:, 0:n], col(V_PSC, l, pc), B[sg_n][:, 0:n], op0=ALU.mult, op1=ALU.mult),
                    reads=[rpsy, rB[sg_n], rVT], writes=[rMIX[pc]])
                rel(rpsy)

        return dict(A=A, B=Bst, C=C, D=Dst)

    def lru_item(l, tl, lc, si, ws, xb):
        n, nt = tl["n"], tl["nt"]
        prompt = tl["kind"] == "p"
        c = lc % 2
        car, rcar = CAR[si], rCAR[si]
        hb, hd = lc % 2, lc % 2
        B = dict(U=HB[hb], XA=car["N0"], SG=car["N1"], TA=HD[hd][0], TX=HD[hd][1], AA=HD[hd][2])
        rB = dict(U=rHB[hb], XA=rcar["N0"], SG=rcar["N1"], TA=rHD[hd][0], TX=rHD[hd][1], AA=rHD[hd][2])
        xsb, rxsb = car["D0"], rcar["D0"]
        wbuf, rw, sw, rsw = WIN[ws], rWIN[ws], SW[ws], rSW[ws]
        st = {}
        U, XA, TA, TX, SG, AA = "U", "XA", "TA", "TX", "SG", "AA"

        def A():
            st["u"] = zmm(wbuf, rw, c, n, xb)
            st["g"] = zmm(wbuf, rw, 2 + c, n, xb)

        def Bst():
            ps, rps = st["u"]
            UV = tview(B[U], tl, HC)
            if prompt:
                T.op("act", lambda e: e.copy(UV[:, 0:HC], HS[:, lc, 15:18]), reads=[rHS[lc]], writes=[rB[U]])
            else:
                T.op("act", lambda e: e.copy(UV[:, :, 0:HC], SSC[:, lc, :, :]), reads=[rSSC[lc]], writes=[rB[U]])
            T.op("act", lambda e: e.copy(tsl(UV, tl, HC, HC + nt), as_t(ps[:, 0:n], tl)), reads=[rps], writes=[rB[U]])
            rel(rps)
            if prompt:
                T.op("pool", lambda e: e.tensor_copy(HS[:, lc, 15:18], UV[:, nt:nt + HC]), reads=[rB[U]], writes=[rHS[lc]])
            else:
                T.op("pool", lambda e: e.tensor_copy(SSC[:, lc, :, :], UV[:, :, nt:nt + HC]), reads=[rB[U]], writes=[rSSC[lc]])
            XV = as_t(B[XA][:, 0:n], tl)
            cw0 = V_CW + l * 32 + lc
            T.op("dve", lambda e: e.tensor_scalar(XV, tsl(UV, tl, 0, nt), VT[:, cw0:cw0 + 1], col(V_CB, l, lc), op0=ALU.mult, op1=ALU.add),
                 reads=[rB[U], rVT], writes=[rB[XA]])
            for tap in range(1, 4):
                cwc = V_CW + l * 32 + tap * 8 + lc
                T.op("dve", lambda e, tap=tap, cwc=cwc: e.scalar_tensor_tensor(
                    XV, tsl(UV, tl, tap, tap + nt), VT[:, cwc:cwc + 1], XV, op0=ALU.mult, op1=ALU.add),
                    reads=[rB[U], rB[XA], rVT], writes=[rB[XA]])
            T.op("dve", lambda e: e.tensor_copy(xsb[:, 0:n], B[XA][:, 0:n]), reads=[rB[XA]], writes=[rxsb])
            psg, rpsg = st["g"]
            T.op("act", lambda e: e.activation(B[SG][:, 0:n], psg[:, 0:n], AF.Silu), reads=[rpsg], writes=[rB[SG]])
            rel(rpsg)

        def C():
            warm(n, xb)
            psa, rpsa = ps_alloc()
            T.op("pe", lambda e: e.matmul(psa[:, 0:n], lhsT=sw[:, c * 128:(c + 1) * 128], rhs=xsb[:, 0:n], start=True, stop=True),
                 reads=rsw + [rxsb], writes=[rpsa])
            psx, rpsx = ps_alloc()
            T.op("pe", lambda e: e.matmul(psx[:, 0:n], lhsT=sw[:, 256 + c * 128:256 + (c + 1) * 128], rhs=xsb[:, 0:n], start=True, stop=True),
                 reads=rsw + [rxsb], writes=[rpsx])
            st["a"] = (psa, rpsa)
            st["x"] = (psx, rpsx)

        def Dst():
            psa, rpsa = st["a"]
            psx, rpsx = st["x"]
            T.op("act", lambda e: e.activation(B[TA][:, 0:n], psa[:, 0:n], AF.Tanh, bias=dcol(0, l, lc), scale=0.5),
                 reads=[rpsa, rDER], writes=[rB[TA]])
            rel(rpsa)
            T.op("act", lambda e: e.activation(B[TX][:, 0:n], psx[:, 0:n], AF.Tanh, bias=dcol(1, l, lc), scale=0.5),
                 reads=[rpsx, rDER], writes=[rB[TX]])
            rel(rpsx)

        def D2():
            T.op("act", lambda e: e.activation(B[AA][:, 0:n], B[TA][:, 0:n], AF.Exp, bias=dcol(3, l, lc), scale=dcol(3, l, lc)),
                 reads=[rB[TA], rDER], writes=[rB[AA]])
            T.op("act", lambda e: e.activation(B[TA][:, 0:n], B[TA][:, 0:n], AF.Exp, bias=dcol(2, l, lc), scale=dcol(2, l, lc)),
                 reads=[rB[TA], rDER], writes=[rB[TA]])
            T.op("act", lambda e: e.activation(B[TA][:, 0:n], B[TA][:, 0:n], AF.Ln, bias=1.0, scale=-1.0), reads=[rB[TA]], writes=[rB[TA]])
            T.op("act", lambda e: e.activation(B[TA][:, 0:n], B[TA][:, 0:n], AF.Exp, scale=0.5), reads=[rB[TA]], writes=[rB[TA]])
            if tl["first"]:
                T.op("dve", lambda e: e.memset(B[TA][:, 0:1], 1.0), writes=[rB[TA]])
            T.op("dve", lambda e: e.scalar_tensor_tensor(B[TX][:, 0:n], B[TX][:, 0:n], 1.0, B[XA][:, 0:n], op0=ALU.add, op1=ALU.mult),
                 reads=[rB[TX], rB[XA]], writes=[rB[TX]])
            T.op("dve", lambda e: e.scalar_tensor_tensor(B[TX][:, 0:n], B[TX][:, 0:n], 0.5, B[TA][:, 0:n], op0=ALU.mult, op1=ALU.mult),
                 reads=[rB[TX], rB[TA]], writes=[rB[TX]])
            if prompt:
                T.op("dve", lambda e: e.tensor_tensor_scan(B[XA][:, 0:n], B[AA][:, 0:n], B[TX][:, 0:n], HS[:, lc, 18:19],
                                                           op0=ALU.mult, op1=ALU.add),
                     reads=[rB[AA], rB[TX], rHS[lc]], writes=[rB[XA]])
                T.op("dve", lambda e: e.tensor_copy(HS[:, lc, 18:19], B[XA][:, n - 1:n]), reads=[rB[XA]], writes=[rHS[lc]])
            else:
                A3 = as_t(B[AA][:, 0:n], tl)
                U3 = as_t(B[TX][:, 0:n], tl)
                H3 = as_t(B[XA][:, 0:n], tl)
                T.op("dve", lambda e: e.tensor_tensor(TMP16, A3[:, :, 0], SSH[:, lc, :], ALU.mult),
                     reads=[rB[AA], rSSH[lc]], writes=[rTMP16])
                T.op("dve", lambda e: e.tensor_tensor(U3[:, :, 0], U3[:, :, 0], TMP16, ALU.add),
                     reads=[rB[TX], rTMP16], writes=[rB[TX]])
                T.op("dve", lambda e: e.memset(A3[:, :, 0], 0.0), writes=[rB[AA]])
                T.op("dve", lambda e: e.tensor_tensor_scan(B[XA][:, 0:n], B[AA][:, 0:n], B[TX][:, 0:n], 0.0, op0=ALU.mult, op1=ALU.add),
                     reads=[rB[AA], rB[TX]], writes=[rB[XA]])
                T.op("dve", lambda e: e.tensor_copy(SSH[:, lc, :], H3[:, :, TS - 1]), reads=[rB[XA]], writes=[rSSH[lc]])
            T.op("dve", lambda e: e.tensor_tensor(MIX[:, 8 + lc, 0:n], B[XA][:, 0:n], B[SG][:, 0:n], ALU.mult),
                 reads=[rB[XA], rB[SG]], writes=[rMIX[8 + lc]])

        return dict(A=A, B=Bst, C=C, D1=Dst, D2=D2, lc=lc)

    def phase3(l, tl, last_layer, sidx3):
        n, t0, ti = tl["n"], tl["t0"], tl["i"]
        psn, rpsn = None, None
        evs = []
        held = {}
        for dc in range(KD):
            ps, rps = ps_alloc()

            pidx = sidx3 * KD + dc
            wslot = pidx % NWO

            def emit(e, ps=ps, dc=dc, wslot=wslot):
                ins = None
                for k in range(16):
                    ins = e.matmul(ps[:, 0:n], lhsT=WOP[wslot][:, k, :], rhs=MIX[:, k, 0:n], start=(k == 0), stop=(k == 15))
                return ins
            T.op("pe", emit, reads=[rWOP[wslot]] + rMIX, writes=[rps])
            issue_wout_piece(pidx + NWO)
            if dc < NOB:
                T.op("act", lambda e, ps=ps, dc=dc: e.copy(OBv[dc][:, 0:n], ps[:, 0:n]), reads=[rps], writes=[rOB[dc]])
            else:
                held[dc] = (ps, rps)
            sq, rsq = sq_alloc()
            T.op("act", lambda e, ps=ps, sq=sq: e.activation(sq[:, 0:n], ps[:, 0:n], AF.Square), reads=[rps], writes=[rsq])
            if dc < NOB:
                rel(rps)
            evs.append((dc, sq, rsq))
            if len(evs) >= 3:
                d0, s0, r0_ = evs[-3]
                if psn is None:
                    psn, rpsn = ps_alloc()
                ones_mm(psn, rpsn, d0, n, s0, r0_)
        for d0, s0, r0_ in evs[-2:]:
            ones_mm(psn, rpsn, d0, n, s0, r0_)
        rstd_from(psn, rpsn, n, RSTD3, rRSTD3)
        for dc in range(KD):
            if dc < NOB:
                src, rsrc, dst, rdst = OBv[dc], rOB[dc], OBv[dc], rOB[dc]
            else:
                (src, rsrc), dst, rdst = held[dc], OBv[dc - NOB], rOB[dc - NOB]
            T.op("dve", lambda e, dc=dc, src=src, dst=dst: e.scalar_tensor_tensor(dst[:, 0:n], src[:, 0:n], col(V_NPOST, l, dc), RSTD3[:, 0:n],
                                                                                  op0=ALU.mult, op1=ALU.mult),
                 reads=[rsrc, rRSTD3, rVT], writes=[rdst])
            if dc >= NOB:
                rel(rsrc)
            T.op("dve", lambda e, dc=dc, dst=dst: e.tensor_tensor(X[:, dc, t0:t0 + n], X[:, dc, t0:t0 + n], dst[:, 0:n], ALU.add),
                 reads=[rX[ti][dc], rdst], writes=[rX[ti][dc]])
        if last_layer:
            for b in range(n // 128):
                c0 = t0 + b * 128
                dst = yp_d[c0:c0 + 128, :] if tl["kind"] == "p" else ys_d
                store_rows(lambda k, c0=c0: X[:, k, c0:c0 + 128], 128, rX[ti], [(0, 128, dst)])

    def win_cols(gi):
        if gi < 4:
            return gi * 256, D + gi * 256
        j = gi - 4
        return 2 * D + j * 256, 3 * D + j * 256

    steps = [(l, tl) for l in range(n_layers) for tl in TILES]
    NG = len(steps) * 8

    def issue_weights(gidx):
        if gidx >= NG:
            return
        l = steps[gidx // 8][0]
        q, odd = (gidx % 8) // 2, gidx % 2
        gi = q if odd else 4 + q
        s = gidx % NWS
        T.dma("pool", WIN[s].rearrange("p k e -> p (k e)"), w_in_d[l, gi], writes=[rWIN[s][0], rWIN[s][1]])
        if gi < 4:
            T.dma("pool", SW[s].rearrange("p (k d) -> p k d", k=2), pool_w_d[l, gi].rearrange("(k p) d -> p k d", p=128), writes=rSW[s])
        else:
            j = gi - 4
            T.dma("pool", SW[s][:, 0:256].rearrange("p (h j) -> p h j", h=2), lru_wa_d[l, 2 * j:2 * j + 2].rearrange("h i j -> i h j"), writes=[rSW[s][0]])
            T.dma("pool", SW[s][:, 256:512].rearrange("p (h j) -> p h j", h=2), lru_wx_d[l, 2 * j:2 * j + 2].rearrange("h i j -> i h j"), writes=[rSW[s][1]])

    def issue_wout_piece(pidx):
        if pidx >= len(steps) * KD:
            return
        l = steps[pidx // KD][0]
        dc = pidx % KD
        T.dma("pool", WOP[pidx % NWO].rearrange("p k d -> p (k d)"), w_out_d[l, dc], writes=[rWOP[pidx % NWO]])

    items = []
    icount = 0
    for sidx, (l, tl) in enumerate(steps):
        xb = sidx % 2
        lst = []
        for q in range(4):
            lst.append(("lru", 2 * q, sidx * 8 + 2 * q))
            lst.append(("lru", 2 * q + 1, sidx * 8 + 2 * q))
            lst.append(("pool", q, sidx * 8 + 2 * q + 1))
        for j, (kind, idx, gidx) in enumerate(lst):
            si = icount % NCAR
            icount += 1
            ws = gidx % NWS
            it = pool_item(l, tl, idx, si, ws, xb) if kind == "pool" else lru_item(l, tl, idx, si, ws, xb)
            it.update(sidx=sidx, j=j, l=l, tl=tl, gidx=gidx, gstart=(kind == "pool" or idx % 2 == 0),
                      glast=(kind == "pool" or idx % 2 == 1), last=(j == len(lst) - 1))
            items.append(it)

    def end_of_step(sidx):
        l, tl = steps[sidx]
        phase3(l, tl, l == n_layers - 1, sidx)
        if tl["last"]:
            store_rows(lambda k: HS[:, k, :], 19, rHS, [(0, 15, npp_d[l]), (15, 18, ncp_d[l]), (18, 19, nlp_d[l])])
        if tl["kind"] == "s":
            for hf in range(2):
                store_rows(lambda k, hf=hf: SSP[:, k, hf * 8:(hf + 1) * 8, :].rearrange("p s r -> p (s r)"), 8 * HP, rSSP,
                           [(0, 8 * HP, nps_d[l, hf * 8 * HP:(hf + 1) * 8 * HP, :])])
            store_rows(lambda k: SSC[:, k, :, :].rearrange("p s r -> p (s r)"), NS * HC, rSSC, [(0, NS * HC, ncs_d[l])])
            store_rows(lambda k: SSH[:, k, :], NS, rSSH, [(0, NS, nls_d[l])])
            if l + 1 < n_layers:
                load_sample_state(l + 1)

    del io_pool[1:]
    issue_weights(0)
    issue_weights(1)
    issue_weights(2)
    for _p in range(NWO):
        issue_wout_piece(_p)
    load_sample_state(0)
    phase1(steps[0][0], steps[0][1], 0)
    bg_on[0] = True
    hist = []
    pending = None

    deferred = []

    def retire_d(old):
        if "D" in old:
            while deferred:
                deferred.pop(0)["D2"]()
            old["D"]()
        elif old["lc"] % 2 == 0:
            old["D1"]()
            deferred.append(old)
        else:
            old["D1"]()
            while deferred:
                deferred.pop(0)["D2"]()
            old["D2"]()

    def retire(old):
        nonlocal pending
        old["C"]()
        if old["glast"]:
            issue_weights(old["gidx"] + NWS)
        retire_d(old)
        if old["last"]:
            pending = old["sidx"]

    for it in items:
        sidx, l, tl = it["sidx"], it["l"], it["tl"]
        if it["j"] == 0 and tl["i"] == 0:
            for k in range(KD):
                T.op("dve", lambda e, k=k: e.memset(HS[:, k, :], 0.0), writes=[rHS[k]])
        it["A"]()
        old = hist[-2] if len(hist) >= 2 else None
        if pending is not None:
            it["B"]()
            end_of_step(pending)
            pending = None
            if old is not None:
                old["C"]()
                if old["glast"]:
                    issue_weights(old["gidx"] + NWS)
        else:
            if old is not None:
                old["C"]()
                if old["glast"]:
                    issue_weights(old["gidx"] + NWS)
            it["B"]()
        if old is not None:
            retire_d(old)
            if old["last"]:
                pending = old["sidx"]
        bg_step(2)
        if sidx + 1 < len(steps):
            nl, ntl = steps[sidx + 1]
            if it["j"] == 4:
                phase1_sq(nl, ntl, range(0, 4))
            elif it["j"] == 5:
                phase1_mm(nl, ntl, range(0, 4))
                phase1_sq(nl, ntl, range(4, 8))
            elif it["j"] == 6:
                phase1_mm(nl, ntl, range(4, 8))
                phase1_fin(nl, ntl, (sidx + 1) % 2)
        hist.append(it)
    for old in hist[-2:]:
        if pending is not None:
            end_of_step(pending)
            pending = None
        retire(old)
    end_of_step(pending)
    for _i in range(2):
        for _j in range(3):
            io_pool.append((HD[_i][_j], rHD[_i][_j]))
    while bg:
        bg_step(1)

    T.finish("sp")
    print("[kernel] ops=%d waits=%d sbuf_left=%d" % (T.nops, T.nwaits, nc.sbuf_bytes_remaining), flush=True)
    return nc


def _host_vecs(inp):
    rows = []
    for nm in ("norm_pre", "norm_post", "pool_scale", "conv_b", "lru_ba", "lru_bx", "lru_lam"):
        rows.append(np.asarray(inp[nm], np.float32).reshape(DEPTH * 8, 128))
    rows.append(np.asarray(inp["conv_w"], np.float32).reshape(DEPTH * 4 * 8, 128))
    return np.ascontiguousarray(np.concatenate(rows, axis=0))


_NC_CACHE = {}


def kernel(**inp):
    n_layers = int(os.environ.get("MK_LAYERS", DEPTH))
    if n_layers not in _NC_CACHE:
        _NC_CACHE[n_layers] = build_program(n_layers)
    nc = _NC_CACHE[n_layers]
    f = lambda a: np.ascontiguousarray(np.asarray(a, np.float32))
    vecs = _host_vecs(inp)
    ident = np.eye(128, dtype=np.float32)
    w_in = f(inp["w_in"]); w_out = f(inp["w_out"])
    grp = []
    for gi in range(8):
        ca, cb = (gi * 256, D + gi * 256) if gi < 4 else (2 * D + (gi - 4) * 256, 3 * D + (gi - 4) * 256)
        g2 = np.concatenate([w_in[:, :, ca:ca + 256], w_in[:, :, cb:cb + 256]], axis=2)
        grp.append(g2.reshape(DEPTH, KD, 128, 512).transpose(0, 2, 1, 3).reshape(DEPTH, 128, KD * 512))
    w_in_r = np.ascontiguousarray(np.stack(grp, axis=1))
    w_out_r = np.ascontiguousarray(w_out.reshape(DEPTH, 16, 128, KD, 128).transpose(0, 3, 2, 1, 4).reshape(DEPTH, KD, 128, 16 * 128))
    shared = dict(vecs=vecs, ident=ident, w_in_r=w_in_r, pool_w=f(inp["pool_w"]), lru_wa=f(inp["lru_wa"]),
                  lru_wx=f(inp["lru_wx"]), w_out_r=w_out_r)
    xp = f(inp["x_prompt"]); xs = f(inp["x_sample"])
    sp = f(inp["state_pool"]); sc = f(inp["state_conv"]); sl = f(inp["state_lru"])
    in_maps = []
    for b in range(NCORES):
        q = slice(b * NS, (b + 1) * NS)
        m = dict(shared)
        m["xp"] = xp[b]
        m["xs"] = xs[q].reshape(NS * TS, D)
        m["spool"] = np.ascontiguousarray(sp[:, q].reshape(DEPTH, NS * HP, D))
        m["sconv"] = np.ascontiguousarray(sc[:, q].reshape(DEPTH, NS * HC, D))
        m["slru"] = np.ascontiguousarray(sl[:, q].reshape(DEPTH, NS, D))
        in_maps.append(m)
    res = run_bass_kernel_spmd(nc, in_maps, core_ids=list(range(NCORES)))
    R = res.results
    g = lambda nm: [np.asarray(R[b][nm], np.float32) for b in range(NCORES)]
    y_prompt = np.stack(g("yp"), axis=0)
    y_sample = np.concatenate([a.reshape(NS, TS, D) for a in g("ys")], axis=0)
    npp = np.stack(g("npp"), axis=1)
    ncp = np.stack(g("ncp"), axis=1)
    nlp = np.stack([a.reshape(DEPTH, D) for a in g("nlp")], axis=1)
    nps = np.concatenate([a.reshape(DEPTH, NS, HP, D) for a in g("nps")], axis=1)
    ncs = np.concatenate([a.reshape(DEPTH, NS, HC, D) for a in g("ncs")], axis=1)
    nls = np.concatenate([a.reshape(DEPTH, NS, D) for a in g("nls")], axis=1)
    return (y_prompt, y_sample, npp, ncp, nlp, nps, ncs, nls)
```

```python
import os
import numpy as np
import concourse.bass as bass
import concourse.mybir as mybir
from concourse.bass_utils import run_bass_kernel_spmd

F32 = mybir.dt.float32
BF16 = mybir.dt.bfloat16
AF = mybir.ActivationFunctionType
ALU = mybir.AluOpType

NCORES = 8
D = 1024
DEPTH = 4
SEQ = 2048
NS = 16
TS = 8
NTOK = SEQ + NS * TS
KD = D // 128
WINS = (2, 4, 8, 16)
EPS = 1e-6
HP = 15
HC = 3
BUFW = 528

V_NPRE, V_NPOST, V_PSC, V_CB, V_BA, V_BX, V_LAM, V_CW = 0, 32, 64, 96, 128, 160, 192, 224
NVEC = 352


class Res:
    __slots__ = ("w", "r")

    def __init__(self):
        self.w = None
        self.r = {}


class Tracker:
    def __init__(self, nc, n_dma_sems=12):
        self.nc = nc
        self.eng = {}
        self.sems = {}
        self.snap = {}
        self.n_dma_sems = n_dma_sems
        self.dpool = {}
        self.dnext = {}
        self.dsem = []
        self.dcount = []
        self.nwaits = 0
        self.nops = 0

    def add_engine(self, name, handle):
        sem = self.nc.alloc_semaphore("e_" + name)
        self.eng[name] = dict(h=handle, sem=sem, count=0, seen={})
        self.sems[name] = sem

    def add_dma_queue(self, name):
        idxs = []
        for i in range(self.n_dma_sems):
            j = len(self.dsem)
            self.dsem.append(self.nc.alloc_semaphore("dq_%s%d" % (name, i)))
            self.dcount.append(0)
            self.sems[("d", j)] = self.dsem[j]
            idxs.append(j)
        self.dpool[name] = idxs
        self.dnext[name] = 0

    def _wait(self, E, key, n):
        if E["seen"].get(key, 0) >= n:
            return
        E["h"].wait_ge(self.sems[key], n)
        self.nwaits += 1
        E["seen"][key] = n
        sn = self.snap.get((key, n))
        if sn:
            for k2, n2 in sn.items():
                if E["seen"].get(k2, 0) < n2:
                    E["seen"][k2] = n2

    def _deps(self, ename, E, reads, writes):
        for r in reads:
            if r.w is not None:
                self._wait(E, r.w[0], r.w[1])
        for w in writes:
            if w.w is not None:
                self._wait(E, w.w[0], w.w[1])
            for k, n in w.r.items():
                self._wait(E, k, n)

    def _commit(self, key, n, reads, writes):
        for r in reads:
            if r.r.get(key, 0) < n:
                r.r[key] = n
        for w in writes:
            w.w = (key, n)
            w.r = {}

    def op(self, ename, emit, reads=(), writes=()):
        E = self.eng[ename]
        self._deps(ename, E, reads, writes)
        ins = emit(E["h"])
        ins.then_inc(E["sem"], 1)
        E["count"] += 1
        n = E["count"]
        sn = {k: v for k, v in E["seen"].items() if not isinstance(k, tuple)}
        sn[ename] = n
        self.snap[(ename, n)] = sn
        self._commit(ename, n, reads, writes)
        self.nops += 1

    def dma(self, ename, out, in_, reads=(), writes=()):
        E = self.eng[ename]
        pool = self.dpool[ename]
        s = pool[self.dnext[ename]]
        self.dnext[ename] = (self.dnext[ename] + 1) % len(pool)
        key = ("d", s)
        if self.dcount[s] > 0:
            self._wait(E, key, self.dcount[s])
        self._deps(ename, E, reads, writes)
        E["h"].dma_start(out=out, in_=in_).then_inc(self.dsem[s], 16)
        self.dcount[s] += 16
        n = self.dcount[s]
        self.snap[(key, n)] = {k: v for k, v in E["seen"].items() if not isinstance(k, tuple)}
        self._commit(key, n, reads, writes)

    def finish(self, ename):
        E = self.eng[ename]
        for s in range(len(self.dsem)):
            if self.dcount[s] > 0:
                self._wait(E, ("d", s), self.dcount[s])
        for k, e2 in self.eng.items():
            if k != ename and e2["count"] > 0:
                self._wait(E, k, e2["count"])


def build_program(n_layers=DEPTH):
    nc = bass.Bass("TRN2", target_bir_lowering=False)
    T = Tracker(nc)
    T.add_engine("pe", nc.tensor)
    T.add_engine("act", nc.scalar)
    T.add_engine("dve", nc.vector)
    T.add_engine("pool", nc.gpsimd)
    T.add_engine("sp", nc.sync)
    T.add_dma_queue("sp")
    T.add_dma_queue("pool")

    def din(name, shape):
        return nc.dram_tensor(name, shape, F32, kind="ExternalInput").ap()

    def dout(name, shape):
        return nc.dram_tensor(name, shape, F32, kind="ExternalOutput").ap()

    xp_d = din("xp", [SEQ, D])
    xs_d = din("xs", [NS * TS, D])
    spool_d = din("spool", [DEPTH, NS * HP, D])
    sconv_d = din("sconv", [DEPTH, NS * HC, D])
    slru_d = din("slru", [DEPTH, NS, D])
    vecs_d = din("vecs", [NVEC, 128])
    ident_d = din("ident", [128, 128])
    w_in_d = din("w_in_r", [DEPTH, 8, 128, KD * 512])
    pool_w_d = din("pool_w", [DEPTH, 4, 256, 256])
    lru_wa_d = din("lru_wa", [DEPTH, 8, 128, 128])
    lru_wx_d = din("lru_wx", [DEPTH, 8, 128, 128])
    w_out_d = din("w_out_r", [DEPTH, KD, 128, 16 * 128])

    wsc_d = nc.dram_tensor("wsc", [8, 128, KD * 512], BF16).ap()
    wosc_d = nc.dram_tensor("wosc", [KD, 128, 16 * 128], BF16).ap()
    rWSC = [Res() for _ in range(8)]
    rWOSC = [Res() for _ in range(KD)]

    yp_d = dout("yp", [SEQ, D])
    ys_d = dout("ys", [NS * TS, D])
    npp_d = dout("npp", [DEPTH, HP, D])
    ncp_d = dout("ncp", [DEPTH, HC, D])
    nlp_d = dout("nlp", [DEPTH, 1, D])
    nps_d = dout("nps", [DEPTH, NS * HP, D])
    ncs_d = dout("ncs", [DEPTH, NS * HC, D])
    nls_d = dout("nls", [DEPTH, NS, D])

    def sb(name, shape, dt=F32):
        return nc.alloc_sbuf_tensor(name, shape, dt).ap()

    TILES = [dict(kind="p", t0=i * 512, n=512, nt=512, first=(i == 0), last=(i == 3)) for i in range(4)]
    TILES.append(dict(kind="s", t0=SEQ, n=NS * TS, nt=TS, first=False, last=False))
    for i, t in enumerate(TILES):
        t["i"] = i
    NT = len(TILES)

    NWS = 3
    X = sb("X", [128, KD, NTOK])
    XN = [sb("XN%d" % i, [128, KD, 512], BF16) for i in range(2)]
    MIX = sb("MIX", [128, 16, 512], BF16)
    NWO = 4
    WOP = [sb("WOP%d" % i, [128, 16, 128], BF16) for i in range(NWO)]
    WIN = [sb("WIN%d" % i, [128, KD, 512], BF16) for i in range(NWS)]
    SW = [sb("SW%d" % i, [128, 512], BF16) for i in range(NWS)]
    VT = sb("VT", [128, NVEC])
    DER = sb("DER", [128, 5, 32])
    IDENT = sb("IDENT", [128, 128])
    ONES = sb("ONES", [128, 128], BF16)
    RC = sb("RC", [128, 4, 16])
    HS = sb("HS", [128, KD, 19])
    SSP = sb("SSP", [128, KD, NS, HP])
    SSC = sb("SSC", [128, KD, NS, HC])
    SSH = sb("SSH", [128, KD, NS])
    IO = sb("IO", [128, 512])
    RSTD1 = sb("RSTD1", [128, 512])
    RSTD3 = RSTD1
    NSQ = 4
    SQ = [sb("SQ%d" % i, [128, 512], BF16) for i in range(NSQ)]
    TMP16 = sb("TMP16", [128, NS])
    NCAR = 4
    CAR = [dict(N0=sb("cN0_%d" % i, [128, 512]), N1=sb("cN1_%d" % i, [128, 512]),
                D0=sb("cD0_%d" % i, [128, 512], BF16)) for i in range(NCAR)]
    PD1 = sb("PD1", [128, 512], BF16)
    rPD1 = Res()
    HB = [sb("HB%d" % i, [128, BUFW]) for i in range(2)]
    HPB = [sb("HPB%d" % i, [128, BUFW]) for i in range(3)]
    HD = [[sb("HD%d_%d" % (i, j), [128, 512]) for j in range(3)] for i in range(2)]
    PS = [nc.alloc_psum_tensor("ps%d" % i, [128, 512], F32).ap() for i in range(8)]

    rX = [[Res() for _ in range(KD)] for _ in range(NT)]
    rXN = [[Res() for _ in range(KD)] for _ in range(2)]
    rMIX = [Res() for _ in range(16)]
    rWOP = [Res() for _ in range(NWO)]
    rWIN = [[Res(), Res()] for _ in range(NWS)]
    rSW = [[Res(), Res()] for _ in range(NWS)]
    rVT, rDER, rIDENT, rONES, rRC, rTMP16 = Res(), Res(), Res(), Res(), Res(), Res()
    rHS = [Res() for _ in range(KD)]
    rSSP = [Res() for _ in range(KD)]
    rSSC = [Res() for _ in range(KD)]
    rSSH = [Res() for _ in range(KD)]
    rIO = Res()
    rRSTD1 = Res()
    rRSTD3 = rRSTD1
    rSQ = [Res() for _ in range(NSQ)]
    rCAR = [dict(N0=Res(), N1=Res(), D0=Res()) for _ in range(NCAR)]
    rHB = [Res() for _ in range(2)]
    rHPB = [Res() for _ in range(3)]
    rHD = [[Res() for _ in range(3)] for _ in range(2)]
    rPS = [Res() for _ in range(8)]
    NOB = 6
    OBv = [HD[dc % 2][dc // 2] for dc in range(NOB)]
    rOB = [rHD[dc % 2][dc // 2] for dc in range(NOB)]
    ps_next = [0]

    ps_live = [False] * 8

    def ps_alloc():
        for _ in range(8):
            i = ps_next[0]
            ps_next[0] = (i + 1) % 8
            if not ps_live[i]:
                ps_live[i] = True
                return PS[i], rPS[i]
        raise RuntimeError("PSUM banks exhausted")

    def rel(rps):
        ps_live[rPS.index(rps)] = False

    NWARM = int(os.environ.get("MK_NWARM", "4"))

    def warm(n, xb):
        if NWARM <= 0:
            return
        ps, rps = ps_alloc()

        def emit(e):
            ins = None
            for k in range(NWARM):
                ins = e.matmul(ps[:, 0:n], lhsT=ONES, rhs=XN[xb][:, k % KD, 0:n], start=(k == 0), stop=(k == NWARM - 1))
            return ins
        T.op("pe", emit, reads=[rONES] + rXN[xb], writes=[rps])
        rel(rps)

    io_pool = [(IO, rIO)]
    io_next = [0]

    def io_alloc():
        i = io_next[0] % len(io_pool)
        io_next[0] += 1
        return io_pool[i]

    def col(base, l, k):
        c = base + l * 8 + k
        return VT[:, c:c + 1]

    def dcol(j, l, k):
        return DER[:, j, l * 8 + k:l * 8 + k + 1]

    def as_t(ap2d, tl):
        if tl["kind"] == "p":
            return ap2d
        return ap2d.rearrange("p (s r) -> p s r", s=NS)

    def tview(buf, tl, H):
        if tl["kind"] == "p":
            return buf[:, 0:H + 512]
        return buf[:, 0:NS * (H + TS)].rearrange("p (s r) -> p s r", s=NS)

    def tsl(view, tl, lo, hi):
        if tl["kind"] == "p":
            return view[:, lo:hi]
        return view[:, :, lo:hi]

    for _i in range(2):
        for _j in range(3):
            io_pool.append((HD[_i][_j], rHD[_i][_j]))
    T.dma("sp", IDENT, ident_d, writes=[rIDENT])
    T.op("pool", lambda e: e.memset(ONES, 1.0), writes=[rONES])
    T.op("pool", lambda e: e.memset(RC, 1.0), writes=[rRC])
    for g, w in enumerate(WINS):
        for t in range(w - 1):
            T.op("pool", lambda e, g=g, t=t: e.memset(RC[:, g, t:t + 1], 1.0 / (t + 1)), writes=[rRC])
    r0 = 0
    while r0 < NVEC:
        nr = min(128, NVEC - r0)
        io, rio = io_alloc()
        T.dma("sp", io[0:nr, 0:128], vecs_d[r0:r0 + nr, :], writes=[rio])
        ps, rps = ps_alloc()
        T.op("pe", lambda e, io=io, nr=nr, ps=ps: e.transpose(ps[:, 0:nr], io[0:nr, 0:128], IDENT[0:nr, 0:nr]),
             reads=[rio, rIDENT], writes=[rps])
        T.op("act", lambda e, ps=ps, r0=r0, nr=nr: e.copy(VT[:, r0:r0 + nr], ps[:, 0:nr]), reads=[rps], writes=[rVT])
        rel(rps)
        r0 += nr
    T.op("dve", lambda e: e.tensor_scalar(DER[:, 0, :], VT[:, V_BA:V_BA + 32], 0.5, None, op0=ALU.mult), reads=[rVT], writes=[rDER])
    T.op("dve", lambda e: e.tensor_scalar(DER[:, 1, :], VT[:, V_BX:V_BX + 32], 0.5, None, op0=ALU.mult), reads=[rVT], writes=[rDER])
    T.op("act", lambda e: e.activation(DER[:, 4, :], VT[:, V_LAM:V_LAM + 32], AF.Exp, scale=-1.0), reads=[rVT], writes=[rDER])
    T.op("act", lambda e: e.activation(DER[:, 4, :], DER[:, 4, :], AF.Ln, bias=1.0, scale=1.0), reads=[rDER], writes=[rDER])
    T.op("dve", lambda e: e.tensor_scalar(DER[:, 2, :], DER[:, 4, :], -8.0, None, op0=ALU.mult), reads=[rDER], writes=[rDER])
    T.op("dve", lambda e: e.tensor_scalar(DER[:, 3, :], DER[:, 4, :], -4.0, None, op0=ALU.mult), reads=[rDER], writes=[rDER])

    bg = []
    bg_on = [False]

    def tr_in(src, nr, dst_fn, rdst):
        for hh in range(2):
            cell = {}

            def t_dma(hh=hh, cell=cell):
                io, rio = io_alloc()
                cell["io"] = (io, rio)
                T.dma("sp", io[0:nr, :], src[:, hh * 512:(hh + 1) * 512], writes=[rio])

            def t_tr(hh=hh, cell=cell):
                io, rio = cell["io"]
                ps, rps = ps_alloc()

                def emit(e):
                    ins = None
                    for j in range(4):
                        ins = e.transpose(ps[:, j * nr:(j + 1) * nr], io[0:nr, j * 128:(j + 1) * 128], IDENT[0:nr, 0:nr])
                    return ins
                T.op("pe", emit, reads=[rio, rIDENT], writes=[rps])
                T.op("act", lambda e: e.copy(dst_fn(hh), ps[:, 0:4 * nr].rearrange("p (j c) -> p j c", j=4)),
                     reads=[rps], writes=[rdst[hh * 4 + j] for j in range(4)])
                rel(rps)
            if bg_on[0]:
                bg.append(t_dma)
                bg.append(t_tr)
            else:
                t_dma()
                t_tr()

    def load_tokens(src_rows, tl_idx, c0):
        tr_in(src_rows, 128, lambda hh: X[:, hh * 4:hh * 4 + 4, c0:c0 + 128], rX[tl_idx])

    for tl in TILES:
        if tl["kind"] == "p":
            for b in range(4):
                load_tokens(xp_d[tl["t0"] + b * 128:tl["t0"] + (b + 1) * 128, :], tl["i"], tl["t0"] + b * 128)
        else:
            load_tokens(xs_d, tl["i"], tl["t0"])

    def load_sample_state(l):
        for hf in range(2):
            nr = 8 * HP
            tr_in(spool_d[l, hf * nr:(hf + 1) * nr, :], nr,
                  lambda hh, hf=hf: SSP[:, hh * 4:hh * 4 + 4, hf * 8:(hf + 1) * 8, :].rearrange("p k s r -> p k (s r)"), rSSP)
        tr_in(sconv_d[l, :, :], NS * HC, lambda hh: SSC[:, hh * 4:hh * 4 + 4, :, :].rearrange("p k s r -> p k (s r)"), rSSC)
        tr_in(slru_d[l, :, :], NS, lambda hh: SSH[:, hh * 4:hh * 4 + 4, :], rSSH)

    def store_rows(src_view_fn, nr, rsrc, dsts):
        for hh in range(2):
            def t_store(hh=hh):
                io, rio = io_alloc()
                ps, rps = ps_alloc()

                def emit(e):
                    ins = None
                    for j in range(4):
                        k = hh * 4 + j
                        ins = e.transpose(ps[0:nr, j * 128:(j + 1) * 128], src_view_fn(k), IDENT)
                    return ins
                T.op("pe", emit, reads=[rsrc[hh * 4 + j] for j in range(4)] + [rIDENT], writes=[rps])
                T.op("act", lambda e: e.copy(io[0:nr, :], ps[0:nr, :]), reads=[rps], writes=[rio])
                rel(rps)
                for (a, b, dst) in dsts:
                    T.dma("sp", dst[:, hh * 512:(hh + 1) * 512], io[a:b, :], reads=[rio])
            if bg_on[0]:
                bg.append(t_store)
            else:
                t_store()

    def bg_step(nmax=1):
        for _ in range(nmax):
            if bg:
                bg.pop(0)()

    def ones_mm(ps, rps, k, n, sq, rsq):
        T.op("pe", lambda e: e.matmul(ps[:, 0:n], lhsT=ONES, rhs=sq[:, 0:n], start=(k == 0), stop=(k == KD - 1)),
             reads=[rsq, rONES], writes=([rps] if k in (0, KD - 1) else []))

    def rstd_from(ps, rps, n, RS, rRS):
        T.op("act", lambda e: e.activation(RS[:, 0:n], ps[:, 0:n], AF.Ln, bias=EPS, scale=1.0 / D), reads=[rps], writes=[rRS])
        rel(rps)
        T.op("act", lambda e: e.activation(RS[:, 0:n], RS[:, 0:n], AF.Exp, scale=-0.5), reads=[rRS], writes=[rRS])

    sq_next = [0]

    def sq_alloc():
        i = sq_next[0]
        sq_next[0] = (i + 1) % NSQ
        return SQ[i], rSQ[i]

    p1 = {}

    def phase1_sq(l, tl, ks):
        n, t0, ti = tl["n"], tl["t0"], tl["i"]
        for k in ks:
            sq, rsq = sq_alloc()
            T.op("act", lambda e, k=k, sq=sq: e.activation(sq[:, 0:n], X[:, k, t0:t0 + n], AF.Square), reads=[rX[ti][k]], writes=[rsq])
            p1[k] = (sq, rsq)

    def phase1_mm(l, tl, ks):
        n = tl["n"]
        if 0 in ks:
            p1["ps"] = ps_alloc()
        ps, rps = p1["ps"]
        for k in ks:
            sq, rsq = p1[k]
            ones_mm(ps, rps, k, n, sq, rsq)

    def phase1_fin(l, tl, xb):
        n, t0, ti = tl["n"], tl["t0"], tl["i"]
        ps, rps = p1["ps"]
        rstd_from(ps, rps, n, RSTD1, rRSTD1)
        for k in range(KD):
            T.op("dve", lambda e, k=k: e.scalar_tensor_tensor(XN[xb][:, k, 0:n], X[:, k, t0:t0 + n], col(V_NPRE, l, k),
                                                              RSTD1[:, 0:n], op0=ALU.mult, op1=ALU.mult),
                 reads=[rX[ti][k], rRSTD1, rVT], writes=[rXN[xb][k]])

    def phase1(l, tl, xb):
        phase1_sq(l, tl, range(0, 4))
        phase1_mm(l, tl, range(0, 4))
        phase1_sq(l, tl, range(4, 8))
        phase1_mm(l, tl, range(4, 8))
        phase1_fin(l, tl, xb)

    def zmm(wbuf, rw, c, n, xb):
        ps, rps = ps_alloc()

        def emit(e, ps=ps):
            ins = None
            for k in range(KD):
                ins = e.matmul(ps[:, 0:n], lhsT=wbuf[:, k, c * 128:(c + 1) * 128], rhs=XN[xb][:, k, 0:n],
                               start=(k == 0), stop=(k == KD - 1))
            return ins
        T.op("pe", emit, reads=[rw[c // 2]] + rXN[xb], writes=[rps])
        return ps, rps

    def pool_item(l, tl, g, si, ws, xb):
        n, nt = tl["n"], tl["nt"]
        prompt = tl["kind"] == "p"
        w = WINS[g]
        m = w.bit_length() - 1
        car, rcar = CAR[si], rCAR[si]
        B = dict(H0=HPB[0], H1=HPB[1], H2=HPB[2], N0=car["N0"], N1=car["N1"])
        rB = dict(H0=rHPB[0], H1=rHPB[1], H2=rHPB[2], N0=rcar["N0"], N1=rcar["N1"])
        dbufs, rdb = [car["D0"], PD1], [rcar["D0"], rPD1]
        wbuf, rw, sw, rsw = WIN[ws], rWIN[ws], SW[ws], rSW[ws]
        st = {}

        def A():
            st["z"] = [zmm(wbuf, rw, c, n, xb) for c in range(4)]

        def chain(c):
            pc = g * 2 + c
            ps, rps = st["z"][c]
            AV = tview(B["H0"], tl, HP)
            if prompt:
                T.op("act", lambda e: e.copy(AV[:, 0:HP], HS[:, pc, 0:HP]), reads=[rHS[pc]], writes=[rB["H0"]])
            else:
                T.op("act", lambda e: e.copy(AV[:, :, 0:HP], SSP[:, pc, :, :]), reads=[rSSP[pc]], writes=[rB["H0"]])
            T.op("act", lambda e: e.copy(tsl(AV, tl, HP, HP + nt), as_t(ps[:, 0:n], tl)), reads=[rps], writes=[rB["H0"]])
            rel(rps)
            if prompt:
                T.op("pool", lambda e: e.tensor_copy(HS[:, pc, 0:HP], AV[:, nt:nt + HP]), reads=[rB["H0"]], writes=[rHS[pc]])
            else:
                T.op("pool", lambda e: e.tensor_copy(SSP[:, pc, :, :], AV[:, :, nt:nt + HP]), reads=[rB["H0"]], writes=[rSSP[pc]])
            lo = [0] * (m + 1)
            lo[m] = HP
            for j in range(m, 0, -1):
                lo[j - 1] = lo[j] - (1 << (j - 1))
            src_n = "H0"
            names = ["H1", "H2"]
            for j in range(1, m + 1):
                dst_n = names[(j - 1) % 2]
                sh = 1 << (j - 1)
                SV = tview(B[src_n], tl, HP)
                DV = tview(B[dst_n], tl, HP)
                T.op("dve", lambda e, SV=SV, DV=DV, j=j, sh=sh: e.tensor_tensor(
                    tsl(DV, tl, lo[j], HP + nt), tsl(SV, tl, lo[j], HP + nt), tsl(SV, tl, lo[j] - sh, HP + nt - sh), ALU.add),
                    reads=[rB[src_n]], writes=[rB[dst_n]])
                src_n = dst_n
            SV = tview(B[src_n], tl, HP)
            T.op("dve", lambda e: e.scalar_tensor_tensor(
                as_t(dbufs[c][:, 0:n], tl), tsl(SV, tl, HP, HP + nt), 1.0 / w, tsl(AV, tl, HP, HP + nt), op0=ALU.mult, op1=ALU.subtract),
                reads=[rB[src_n], rB["H0"]], writes=[rdb[c]])
            if tl["first"]:
                tmp_n = names[m % 2]
                TV = B[tmp_n]
                T.op("dve", lambda e: e.tensor_tensor(TV[:, 0:w - 1], SV[:, HP:HP + w - 1], RC[:, g, 0:w - 1], ALU.mult),
                     reads=[rB[src_n], rRC], writes=[rB[tmp_n]])
                T.op("dve", lambda e: e.tensor_tensor(dbufs[c][:, 0:w - 1], TV[:, 0:w - 1], AV[:, HP:HP + w - 1], ALU.subtract),
                     reads=[rB[tmp_n], rB["H0"]], writes=[rdb[c]])

        def Bst():
            chain(0)
            for mo in range(2):
                psg, rpsg = st["z"][2 + mo]
                sg_n = ("N0", "N1")[mo]
                T.op("act", lambda e, psg=psg, sg_n=sg_n: e.activation(B[sg_n][:, 0:n], psg[:, 0:n], AF.Silu), reads=[rpsg], writes=[rB[sg_n]])
                rel(rpsg)
            chain(1)

        def C():
            warm(n, xb)
            st["y"] = []
            for mo in range(2):
                psy, rpsy = ps_alloc()

                def emit(e, psy=psy, mo=mo):
                    ins = None
                    for k in range(2):
                        ins = e.matmul(psy[:, 0:n], lhsT=sw[:, k * 256 + mo * 128:k * 256 + (mo + 1) * 128], rhs=dbufs[k][:, 0:n],
                                       start=(k == 0), stop=(k == 1))
                    return ins
                T.op("pe", emit, reads=rsw + [rdb[0], rdb[1]], writes=[rpsy])
                st["y"].append((psy, rpsy))

        def Dst():
            for mo in range(2):
                pc = g * 2 + mo
                psy, rpsy = st["y"][mo]
                sg_n = ("N0", "N1")[mo]
                T.op("dve", lambda e, psy=psy, sg_n=sg_n, pc=pc: e.scalar_tensor_tensor(
                    MIX[:, pc, 0:n], psy[:, 0:n], col(V_PSC, l, pc), B[sg_n][:, 0:n], op0=ALU.mult, op1=ALU.mult),
                    reads=[rpsy, rB[sg_n], rVT], writes=[rMIX[pc]])
                rel(rpsy)

        return dict(A=A, B=Bst, C=C, D=Dst)

    def lru_item(l, tl, lc, si, ws, xb):
        n, nt = tl["n"], tl["nt"]
        prompt = tl["kind"] == "p"
        c = lc % 2
        car, rcar = CAR[si], rCAR[si]
        hb, hd = lc % 2, lc % 2
        B = dict(U=HB[hb], XA=car["N0"], SG=car["N1"], TA=HD[hd][0], TX=HD[hd][1], AA=HD[hd][2])
        rB = dict(U=rHB[hb], XA=rcar["N0"], SG=rcar["N1"], TA=rHD[hd][0], TX=rHD[hd][1], AA=rHD[hd][2])
        xsb, rxsb = car["D0"], rcar["D0"]
        wbuf, rw, sw, rsw = WIN[ws], rWIN[ws], SW[ws], rSW[ws]
        st = {}
        U, XA, TA, TX, SG, AA = "U", "XA", "TA", "TX", "SG", "AA"

        def A():
            st["u"] = zmm(wbuf, rw, c, n, xb)
            st["g"] = zmm(wbuf, rw, 2 + c, n, xb)

        def Bst():
            ps, rps = st["u"]
            UV = tview(B[U], tl, HC)
            if prompt:
                T.op("act", lambda e: e.copy(UV[:, 0:HC], HS[:, lc, 15:18]), reads=[rHS[lc]], writes=[rB[U]])
            else:
                T.op("act", lambda e: e.copy(UV[:, :, 0:HC], SSC[:, lc, :, :]), reads=[rSSC[lc]], writes=[rB[U]])
            T.op("act", lambda e: e.copy(tsl(UV, tl, HC, HC + nt), as_t(ps[:, 0:n], tl)), reads=[rps], writes=[rB[U]])
            rel(rps)
            if prompt:
                T.op("pool", lambda e: e.tensor_copy(HS[:, lc, 15:18], UV[:, nt:nt + HC]), reads=[rB[U]], writes=[rHS[lc]])
            else:
                T.op("pool", lambda e: e.tensor_copy(SSC[:, lc, :, :], UV[:, :, nt:nt + HC]), reads=[rB[U]], writes=[rSSC[lc]])
            XV = as_t(B[XA][:, 0:n], tl)
            cw0 = V_CW + l * 32 + lc
            T.op("dve", lambda e: e.tensor_scalar(XV, tsl(UV, tl, 0, nt), VT[:, cw0:cw0 + 1], col(V_CB, l, lc), op0=ALU.mult, op1=ALU.add),
                 reads=[rB[U], rVT], writes=[rB[XA]])
            for tap in range(1, 4):
                cwc = V_CW + l * 32 + tap * 8 + lc
                T.op("dve", lambda e, tap=tap, cwc=cwc: e.scalar_tensor_tensor(
                    XV, tsl(UV, tl, tap, tap + nt), VT[:, cwc:cwc + 1], XV, op0=ALU.mult, op1=ALU.add),
                    reads=[rB[U], rB[XA], rVT], writes=[rB[XA]])
            T.op("dve", lambda e: e.tensor_copy(xsb[:, 0:n], B[XA][:, 0:n]), reads=[rB[XA]], writes=[rxsb])
            psg, rpsg = st["g"]
            T.op("act", lambda e: e.activation(B[SG][:, 0:n], psg[:, 0:n], AF.Silu), reads=[rpsg], writes=[rB[SG]])
            rel(rpsg)

        def C():
            warm(n, xb)
            psa, rpsa = ps_alloc()
            T.op("pe", lambda e: e.matmul(psa[:, 0:n], lhsT=sw[:, c * 128:(c + 1) * 128], rhs=xsb[:, 0:n], start=True, stop=True),
                 reads=rsw + [rxsb], writes=[rpsa])
            psx, rpsx = ps_alloc()
            T.op("pe", lambda e: e.matmul(psx[:, 0:n], lhsT=sw[:, 256 + c * 128:256 + (c + 1) * 128], rhs=xsb[:, 0:n], start=True, stop=True),
                 reads=rsw + [rxsb], writes=[rpsx])
            st["a"] = (psa, rpsa)
            st["x"] = (psx, rpsx)

        def Dst():
            psa, rpsa = st["a"]
            psx, rpsx = st["x"]
            T.op("act", lambda e: e.activation(B[TA][:, 0:n], psa[:, 0:n], AF.Tanh, bias=dcol(0, l, lc), scale=0.5),
                 reads=[rpsa, rDER], writes=[rB[TA]])
            rel(rpsa)
            T.op("act", lambda e: e.activation(B[TX][:, 0:n], psx[:, 0:n], AF.Tanh, bias=dcol(1, l, lc), scale=0.5),
                 reads=[rpsx, rDER], writes=[rB[TX]])
            rel(rpsx)

        def D2():
            T.op("act", lambda e: e.activation(B[AA][:, 0:n], B[TA][:, 0:n], AF.Exp, bias=dcol(3, l, lc), scale=dcol(3, l, lc)),
                 reads=[rB[TA], rDER], writes=[rB[AA]])
            T.op("act", lambda e: e.activation(B[TA][:, 0:n], B[TA][:, 0:n], AF.Exp, bias=dcol(2, l, lc), scale=dcol(2, l, lc)),
                 reads=[rB[TA], rDER], writes=[rB[TA]])
            T.op("act", lambda e: e.activation(B[TA][:, 0:n], B[TA][:, 0:n], AF.Ln, bias=1.0, scale=-1.0), reads=[rB[TA]], writes=[rB[TA]])
            T.op("act", lambda e: e.activation(B[TA][:, 0:n], B[TA][:, 0:n], AF.Exp, scale=0.5), reads=[rB[TA]], writes=[rB[TA]])
            if tl["first"]:
                T.op("dve", lambda e: e.memset(B[TA][:, 0:1], 1.0), writes=[rB[TA]])
            T.op("dve", lambda e: e.scalar_tensor_tensor(B[TX][:, 0:n], B[TX][:, 0:n], 1.0, B[XA][:, 0:n], op0=ALU.add, op1=ALU.mult),
                 reads=[rB[TX], rB[XA]], writes=[rB[TX]])
            T.op("dve", lambda e: e.scalar_tensor_tensor(B[TX][:, 0:n], B[TX][:, 0:n], 0.5, B[TA][:, 0:n], op0=ALU.mult, op1=ALU.mult),
                 reads=[rB[TX], rB[TA]], writes=[rB[TX]])
            if prompt:
                T.op("dve", lambda e: e.tensor_tensor_scan(B[XA][:, 0:n], B[AA][:, 0:n], B[TX][:, 0:n], HS[:, lc, 18:19],
                                                           op0=ALU.mult, op1=ALU.add),
                     reads=[rB[AA], rB[TX], rHS[lc]], writes=[rB[XA]])
                T.op("dve", lambda e: e.tensor_copy(HS[:, lc, 18:19], B[XA][:, n - 1:n]), reads=[rB[XA]], writes=[rHS[lc]])
            else:
                A3 = as_t(B[AA][:, 0:n], tl)
                U3 = as_t(B[TX][:, 0:n], tl)
                H3 = as_t(B[XA][:, 0:n], tl)
                T.op("dve", lambda e: e.tensor_tensor(TMP16, A3[:, :, 0], SSH[:, lc, :], ALU.mult),
                     reads=[rB[AA], rSSH[lc]], writes=[rTMP16])
                T.op("dve", lambda e: e.tensor_tensor(U3[:, :, 0], U3[:, :, 0], TMP16, ALU.add),
                     reads=[rB[TX], rTMP16], writes=[rB[TX]])
                T.op("dve", lambda e: e.memset(A3[:, :, 0], 0.0), writes=[rB[AA]])
                T.op("dve", lambda e: e.tensor_tensor_scan(B[XA][:, 0:n], B[AA][:, 0:n], B[TX][:, 0:n], 0.0, op0=ALU.mult, op1=ALU.add),
                     reads=[rB[AA], rB[TX]], writes=[rB[XA]])
                T.op("dve", lambda e: e.tensor_copy(SSH[:, lc, :], H3[:, :, TS - 1]), reads=[rB[XA]], writes=[rSSH[lc]])
            T.op("dve", lambda e: e.tensor_tensor(MIX[:, 8 + lc, 0:n], B[XA][:, 0:n], B[SG][:, 0:n], ALU.mult),
                 reads=[rB[XA], rB[SG]], writes=[rMIX[8 + lc]])

        return dict(A=A, B=Bst, C=C, D1=Dst, D2=D2, lc=lc)

    def phase3(l, tl, last_layer, sidx3):
        n, t0, ti = tl["n"], tl["t0"], tl["i"]
        psn, rpsn = None, None
        evs = []
        held = {}
        for dc in range(KD):
            ps, rps = ps_alloc()

            pidx = sidx3 * KD + dc
            wslot = pidx % NWO

            def emit(e, ps=ps, dc=dc, wslot=wslot):
                ins = None
                for k in range(16):
                    ins = e.matmul(ps[:, 0:n], lhsT=WOP[wslot][:, k, :], rhs=MIX[:, k, 0:n], start=(k == 0), stop=(k == 15))
                return ins
            T.op("pe", emit, reads=[rWOP[wslot]] + rMIX, writes=[rps])
            issue_wout_piece(pidx + NWO)
            if dc < NOB:
                T.op("act", lambda e, ps=ps, dc=dc: e.copy(OBv[dc][:, 0:n], ps[:, 0:n]), reads=[rps], writes=[rOB[dc]])
            else:
                held[dc] = (ps, rps)
            sq, rsq = sq_alloc()
            T.op("act", lambda e, ps=ps, sq=sq: e.activation(sq[:, 0:n], ps[:, 0:n], AF.Square), reads=[rps], writes=[rsq])
            if dc < NOB:
                rel(rps)
            evs.append((dc, sq, rsq))
            if len(evs) >= 3:
                d0, s0, r0_ = evs[-3]
                if psn is None:
                    psn, rpsn = ps_alloc()
                ones_mm(psn, rpsn, d0, n, s0, r0_)
        for d0, s0, r0_ in evs[-2:]:
            ones_mm(psn, rpsn, d0, n, s0, r0_)
        rstd_from(psn, rpsn, n, RSTD3, rRSTD3)
        for dc in range(KD):
            if dc < NOB:
                src, rsrc, dst, rdst = OBv[dc], rOB[dc], OBv[dc], rOB[dc]
            else:
                (src, rsrc), dst, rdst = held[dc], OBv[dc - NOB], rOB[dc - NOB]
            T.op("dve", lambda e, dc=dc, src=src, dst=dst: e.scalar_tensor_tensor(dst[:, 0:n], src[:, 0:n], col(V_NPOST, l, dc), RSTD3[:, 0:n],
                                                                                  op0=ALU.mult, op1=ALU.mult),
                 reads=[rsrc, rRSTD3, rVT], writes=[rdst])
            if dc >= NOB:
                rel(rsrc)
            T.op("dve", lambda e, dc=dc, dst=dst: e.tensor_tensor(X[:, dc, t0:t0 + n], X[:, dc, t0:t0 + n], dst[:, 0:n], ALU.add),
                 reads=[rX[ti][dc], rdst], writes=[rX[ti][dc]])
        if last_layer:
            for b in range(n // 128):
                c0 = t0 + b * 128
                dst = yp_d[c0:c0 + 128, :] if tl["kind"] == "p" else ys_d
                store_rows(lambda k, c0=c0: X[:, k, c0:c0 + 128], 128, rX[ti], [(0, 128, dst)])

    def win_cols(gi):
        if gi < 4:
            return gi * 256, D + gi * 256
        j = gi - 4
        return 2 * D + j * 256, 3 * D + j * 256

    steps = [(l, tl) for l in range(n_layers) for tl in TILES]
    NG = len(steps) * 8

    def issue_weights(gidx):
        if gidx >= NG:
            return
        l = steps[gidx // 8][0]
        q, odd = (gidx % 8) // 2, gidx % 2
        gi = q if odd else 4 + q
        s = gidx % NWS
        if steps[gidx // 8][1]["i"] == 0:
            T.dma("pool", WIN[s].rearrange("p k e -> p (k e)"), w_in_d[l, gi], writes=[rWIN[s][0], rWIN[s][1]])
            T.dma("sp", wsc_d[gi], WIN[s].rearrange("p k e -> p (k e)"), reads=[rWIN[s][0], rWIN[s][1]], writes=[rWSC[gi]])
        else:
            T.dma("pool", WIN[s].rearrange("p k e -> p (k e)"), wsc_d[gi], reads=[rWSC[gi]], writes=[rWIN[s][0], rWIN[s][1]])
        if gi < 4:
            T.dma("pool", SW[s].rearrange("p (k d) -> p k d", k=2), pool_w_d[l, gi].rearrange("(k p) d -> p k d", p=128), writes=rSW[s])
        else:
            j = gi - 4
            T.dma("pool", SW[s][:, 0:256].rearrange("p (h j) -> p h j", h=2), lru_wa_d[l, 2 * j:2 * j + 2].rearrange("h i j -> i h j"), writes=[rSW[s][0]])
            T.dma("pool", SW[s][:, 256:512].rearrange("p (h j) -> p h j", h=2), lru_wx_d[l, 2 * j:2 * j + 2].rearrange("h i j -> i h j"), writes=[rSW[s][1]])

    def issue_wout_piece(pidx):
        if pidx >= len(steps) * KD:
            return
        l = steps[pidx // KD][0]
        dc = pidx % KD
        ws = pidx % NWO
        if steps[pidx // KD][1]["i"] == 0:
            T.dma("pool", WOP[ws].rearrange("p k d -> p (k d)"), w_out_d[l, dc], writes=[rWOP[ws]])
            T.dma("sp", wosc_d[dc], WOP[ws].rearrange("p k d -> p (k d)"), reads=[rWOP[ws]], writes=[rWOSC[dc]])
        else:
            T.dma("pool", WOP[ws].rearrange("p k d -> p (k d)"), wosc_d[dc], reads=[rWOSC[dc]], writes=[rWOP[ws]])

    items = []
    icount = 0
    for sidx, (l, tl) in enumerate(steps):
        xb = sidx % 2
        lst = []
        for q in range(4):
            lst.append(("lru", 2 * q, sidx * 8 + 2 * q))
            lst.append(("lru", 2 * q + 1, sidx * 8 + 2 * q))
            lst.append(("pool", q, sidx * 8 + 2 * q + 1))
        for j, (kind, idx, gidx) in enumerate(lst):
            si = icount % NCAR
            icount += 1
            ws = gidx % NWS
            it = pool_item(l, tl, idx, si, ws, xb) if kind == "pool" else lru_item(l, tl, idx, si, ws, xb)
            it.update(sidx=sidx, j=j, l=l, tl=tl, gidx=gidx, gstart=(kind == "pool" or idx % 2 == 0),
                      glast=(kind == "pool" or idx % 2 == 1), last=(j == len(lst) - 1))
            items.append(it)

    def end_of_step(sidx):
        l, tl = steps[sidx]
        phase3(l, tl, l == n_layers - 1, sidx)
        if tl["last"]:
            store_rows(lambda k: HS[:, k, :], 19, rHS, [(0, 15, npp_d[l]), (15, 18, ncp_d[l]), (18, 19, nlp_d[l])])
        if tl["kind"] == "s":
            for hf in range(2):
                store_rows(lambda k, hf=hf: SSP[:, k, hf * 8:(hf + 1) * 8, :].rearrange("p s r -> p (s r)"), 8 * HP, rSSP,
                           [(0, 8 * HP, nps_d[l, hf * 8 * HP:(hf + 1) * 8 * HP, :])])
            store_rows(lambda k: SSC[:, k, :, :].rearrange("p s r -> p (s r)"), NS * HC, rSSC, [(0, NS * HC, ncs_d[l])])
            store_rows(lambda k: SSH[:, k, :], NS, rSSH, [(0, NS, nls_d[l])])
            if l + 1 < n_layers:
                load_sample_state(l + 1)

    del io_pool[1:]
    issue_weights(0)
    issue_weights(1)
    issue_weights(2)
    for _p in range(NWO):
        issue_wout_piece(_p)
    load_sample_state(0)
    phase1(steps[0][0], steps[0][1], 0)
    bg_on[0] = True
    hist = []
    pending = None

    deferred = []

    def retire_d(old):
        if "D" in old:
            while deferred:
                deferred.pop(0)["D2"]()
            old["D"]()
        elif old["lc"] % 2 == 0:
            old["D1"]()
            deferred.append(old)
        else:
            old["D1"]()
            while deferred:
                deferred.pop(0)["D2"]()
            old["D2"]()

    def retire(old):
        nonlocal pending
        old["C"]()
        if old["glast"]:
            issue_weights(old["gidx"] + NWS)
        retire_d(old)
        if old["last"]:
            pending = old["sidx"]

    for it in items:
        sidx, l, tl = it["sidx"], it["l"], it["tl"]
        if it["j"] == 0 and tl["i"] == 0:
            for k in range(KD):
                T.op("dve", lambda e, k=k: e.memset(HS[:, k, :], 0.0), writes=[rHS[k]])
        it["A"]()
        old = hist[-2] if len(hist) >= 2 else None
        if pending is not None:
            it["B"]()
            end_of_step(pending)
            pending = None
            if old is not None:
                old["C"]()
                if old["glast"]:
                    issue_weights(old["gidx"] + NWS)
        else:
            if old is not None:
                old["C"]()
                if old["glast"]:
                    issue_weights(old["gidx"] + NWS)
            it["B"]()
        if old is not None:
            retire_d(old)
            if old["last"]:
                pending = old["sidx"]
        bg_step(1)
        if sidx + 1 < len(steps):
            nl, ntl = steps[sidx + 1]
            if it["j"] == 4:
                phase1_sq(nl, ntl, range(0, 4))
            elif it["j"] == 5:
                phase1_mm(nl, ntl, range(0, 4))
                phase1_sq(nl, ntl, range(4, 8))
            elif it["j"] == 6:
                phase1_mm(nl, ntl, range(4, 8))
                phase1_fin(nl, ntl, (sidx + 1) % 2)
        hist.append(it)
    for old in hist[-2:]:
        if pending is not None:
            end_of_step(pending)
            pending = None
        retire(old)
    end_of_step(pending)
    for _i in range(2):
        for _j in range(3):
            io_pool.append((HD[_i][_j], rHD[_i][_j]))
    while bg:
        bg_step(1)

    T.finish("sp")
    print("[kernel] ops=%d waits=%d sbuf_left=%d" % (T.nops, T.nwaits, nc.sbuf_bytes_remaining), flush=True)
    return nc


def _host_vecs(inp):
    rows = []
    for nm in ("norm_pre", "norm_post", "pool_scale", "conv_b", "lru_ba", "lru_bx", "lru_lam"):
        rows.append(np.asarray(inp[nm], np.float32).reshape(DEPTH * 8, 128))
    rows.append(np.asarray(inp["conv_w"], np.float32).reshape(DEPTH * 4 * 8, 128))
    return np.ascontiguousarray(np.concatenate(rows, axis=0))


_NC_CACHE = {}


def kernel(**inp):
    n_layers = int(os.environ.get("MK_LAYERS", DEPTH))
    if n_layers not in _NC_CACHE:
        _NC_CACHE[n_layers] = build_program(n_layers)
    nc = _NC_CACHE[n_layers]
    f = lambda a: np.ascontiguousarray(np.asarray(a, np.float32))
    vecs = _host_vecs(inp)
    ident = np.eye(128, dtype=np.float32)
    w_in = f(inp["w_in"]); w_out = f(inp["w_out"])
    grp = []
    for gi in range(8):
        ca, cb = (gi * 256, D + gi * 256) if gi < 4 else (2 * D + (gi - 4) * 256, 3 * D + (gi - 4) * 256)
        g2 = np.concatenate([w_in[:, :, ca:ca + 256], w_in[:, :, cb:cb + 256]], axis=2)
        grp.append(g2.reshape(DEPTH, KD, 128, 512).transpose(0, 2, 1, 3).reshape(DEPTH, 128, KD * 512))
    w_in_r = np.ascontiguousarray(np.stack(grp, axis=1))
    w_out_r = np.ascontiguousarray(w_out.reshape(DEPTH, 16, 128, KD, 128).transpose(0, 3, 2, 1, 4).reshape(DEPTH, KD, 128, 16 * 128))
    shared = dict(vecs=vecs, ident=ident, w_in_r=w_in_r, pool_w=f(inp["pool_w"]), lru_wa=f(inp["lru_wa"]),
                  lru_wx=f(inp["lru_wx"]), w_out_r=w_out_r)
    xp = f(inp["x_prompt"]); xs = f(inp["x_sample"])
    sp = f(inp["state_pool"]); sc = f(inp["state_conv"]); sl = f(inp["state_lru"])
    in_maps = []
    for b in range(NCORES):
        q = slice(b * NS, (b + 1) * NS)
        m = dict(shared)
        m["xp"] = xp[b]
        m["xs"] = xs[q].reshape(NS * TS, D)
        m["spool"] = np.ascontiguousarray(sp[:, q].reshape(DEPTH, NS * HP, D))
        m["sconv"] = np.ascontiguousarray(sc[:, q].reshape(DEPTH, NS * HC, D))
        m["slru"] = np.ascontiguousarray(sl[:, q].reshape(DEPTH, NS, D))
        in_maps.append(m)
    res = run_bass_kernel_spmd(nc, in_maps, core_ids=list(range(NCORES)))
    R = res.results
    g = lambda nm: [np.asarray(R[b][nm], np.float32) for b in range(NCORES)]
    y_prompt = np.stack(g("yp"), axis=0)
    y_sample = np.concatenate([a.reshape(NS, TS, D) for a in g("ys")], axis=0)
    npp = np.stack(g("npp"), axis=1)
    ncp = np.stack(g("ncp"), axis=1)
    nlp = np.stack([a.reshape(DEPTH, D) for a in g("nlp")], axis=1)
    nps = np.concatenate([a.reshape(DEPTH, NS, HP, D) for a in g("nps")], axis=1)
    ncs = np.concatenate([a.reshape(DEPTH, NS, HC, D) for a in g("ncs")], axis=1)
    nls = np.concatenate([a.reshape(DEPTH, NS, D) for a in g("nls")], axis=1)
    return (y_prompt, y_sample, npp, ncp, nlp, nps, ncs, nls)
```

```python
import os
import numpy as np
import concourse.bass as bass
import concourse.mybir as mybir
from concourse.bass_utils import run_bass_kernel_spmd

F32 = mybir.dt.float32
BF16 = mybir.dt.bfloat16
AF = mybir.ActivationFunctionType
ALU = mybir.AluOpType

NCORES = 8
D = 1024
DEPTH = 4
SEQ = 2048
NS = 16
TS = 8
NTOK = SEQ + NS * TS
KD = D // 128
WINS = (2, 4, 8, 16)
EPS = 1e-6
HP = 15
HC = 3
BUFW = 528

V_NPRE, V_NPOST, V_PSC, V_CB, V_BA, V_BX, V_LAM, V_CW = 0, 32, 64, 96, 128, 160, 192, 224
NVEC = 352


class Res:
    __slots__ = ("w", "r")

    def __init__(self):
        self.w = None
        self.r = {}


class Tracker:
    def __init__(self, nc, n_dma_sems=12):
        self.nc = nc
        self.eng = {}
        self.sems = {}
        self.snap = {}
        self.n_dma_sems = n_dma_sems
        self.dpool = {}
        self.dnext = {}
        self.dsem = []
        self.dcount = []
        self.nwaits = 0
        self.nops = 0

    def add_engine(self, name, handle):
        sem = self.nc.alloc_semaphore("e_" + name)
        self.eng[name] = dict(h=handle, sem=sem, count=0, seen={})
        self.sems[name] = sem

    def add_dma_queue(self, name):
        idxs = []
        for i in range(self.n_dma_sems):
            j = len(self.dsem)
            self.dsem.append(self.nc.alloc_semaphore("dq_%s%d" % (name, i)))
            self.dcount.append(0)
            self.sems[("d", j)] = self.dsem[j]
            idxs.append(j)
        self.dpool[name] = idxs
        self.dnext[name] = 0

    def _wait(self, E, key, n):
        if E["seen"].get(key, 0) >= n:
            return
        E["h"].wait_ge(self.sems[key], n)
        self.nwaits += 1
        E["seen"][key] = n
        sn = self.snap.get((key, n))
        if sn:
            for k2, n2 in sn.items():
                if E["seen"].get(k2, 0) < n2:
                    E["seen"][k2] = n2

    def _deps(self, ename, E, reads, writes):
        for r in reads:
            if r.w is not None:
                self._wait(E, r.w[0], r.w[1])
        for w in writes:
            if w.w is not None:
                self._wait(E, w.w[0], w.w[1])
            for k, n in w.r.items():
                self._wait(E, k, n)

    def _commit(self, key, n, reads, writes):
        for r in reads:
            if r.r.get(key, 0) < n:
                r.r[key] = n
        for w in writes:
            w.w = (key, n)
            w.r = {}

    def op(self, ename, emit, reads=(), writes=()):
        E = self.eng[ename]
        self._deps(ename, E, reads, writes)
        ins = emit(E["h"])
        ins.then_inc(E["sem"], 1)
        E["count"] += 1
        n = E["count"]
        sn = {k: v for k, v in E["seen"].items() if not isinstance(k, tuple)}
        sn[ename] = n
        self.snap[(ename, n)] = sn
        self._commit(ename, n, reads, writes)
        self.nops += 1

    def dma(self, ename, out, in_, reads=(), writes=()):
        E = self.eng[ename]
        pool = self.dpool[ename]
        s = pool[self.dnext[ename]]
        self.dnext[ename] = (self.dnext[ename] + 1) % len(pool)
        key = ("d", s)
        if self.dcount[s] > 0:
            self._wait(E, key, self.dcount[s])
        self._deps(ename, E, reads, writes)
        E["h"].dma_start(out=out, in_=in_).then_inc(self.dsem[s], 16)
        self.dcount[s] += 16
        n = self.dcount[s]
        self.snap[(key, n)] = {k: v for k, v in E["seen"].items() if not isinstance(k, tuple)}
        self._commit(key, n, reads, writes)

    def finish(self, ename):
        E = self.eng[ename]
        for s in range(len(self.dsem)):
            if self.dcount[s] > 0:
                self._wait(E, ("d", s), self.dcount[s])
        for k, e2 in self.eng.items():
            if k != ename and e2["count"] > 0:
                self._wait(E, k, e2["count"])


def build_program(n_layers=DEPTH):
    nc = bass.Bass("TRN2", target_bir_lowering=False)
    T = Tracker(nc)
    T.add_engine("pe", nc.tensor)
    T.add_engine("act", nc.scalar)
    T.add_engine("dve", nc.vector)
    T.add_engine("pool", nc.gpsimd)
    T.add_engine("sp", nc.sync)
    T.add_dma_queue("sp")
    T.add_dma_queue("pool")

    def din(name, shape):
        return nc.dram_tensor(name, shape, F32, kind="ExternalInput").ap()

    def dout(name, shape):
        return nc.dram_tensor(name, shape, F32, kind="ExternalOutput").ap()

    xp_d = din("xp", [SEQ, D])
    xs_d = din("xs", [NS * TS, D])
    spool_d = din("spool", [DEPTH, NS * HP, D])
    sconv_d = din("sconv", [DEPTH, NS * HC, D])
    slru_d = din("slru", [DEPTH, NS, D])
    vecs_d = din("vecs", [NVEC, 128])
    ident_d = din("ident", [128, 128])
    w_in_d = din("w_in_r", [DEPTH, 8, 128, KD * 512])
    pool_w_d = din("pool_w", [DEPTH, 4, 256, 256])
    lru_wa_d = din("lru_wa", [DEPTH, 8, 128, 128])
    lru_wx_d = din("lru_wx", [DEPTH, 8, 128, 128])
    w_out_d = din("w_out_r", [DEPTH, KD, 128, 16 * 128])

    yp_d = dout("yp", [SEQ, D])
    ys_d = dout("ys", [NS * TS, D])
    npp_d = dout("npp", [DEPTH, HP, D])
    ncp_d = dout("ncp", [DEPTH, HC, D])
    nlp_d = dout("nlp", [DEPTH, 1, D])
    nps_d = dout("nps", [DEPTH, NS * HP, D])
    ncs_d = dout("ncs", [DEPTH, NS * HC, D])
    nls_d = dout("nls", [DEPTH, NS, D])

    def sb(name, shape, dt=F32):
        return nc.alloc_sbuf_tensor(name, shape, dt).ap()

    TILES = [dict(kind="p", t0=i * 512, n=512, nt=512, first=(i == 0), last=(i == 3)) for i in range(4)]
    TILES.append(dict(kind="s", t0=SEQ, n=NS * TS, nt=TS, first=False, last=False))
    for i, t in enumerate(TILES):
        t["i"] = i
    NT = len(TILES)

    NWS = 3
    X = sb("X", [128, KD, NTOK])
    XN = [sb("XN%d" % i, [128, KD, 512], BF16) for i in range(2)]
    MIX = sb("MIX", [128, 16, 512], BF16)
    NWO = 4
    WOP = [sb("WOP%d" % i, [128, 16, 128], BF16) for i in range(NWO)]
    WIN = [sb("WIN%d" % i, [128, KD, 512], BF16) for i in range(NWS)]
    SW = [sb("SW%d" % i, [128, 512], BF16) for i in range(NWS)]
    VT = sb("VT", [128, NVEC])
    DER = sb("DER", [128, 5, 32])
    IDENT = sb("IDENT", [128, 128])
    ONES = sb("ONES", [128, 128], BF16)
    RC = sb("RC", [128, 4, 16])
    HS = sb("HS", [128, KD, 19])
    SSP = sb("SSP", [128, KD, NS, HP])
    SSC = sb("SSC", [128, KD, NS, HC])
    SSH = sb("SSH", [128, KD, NS])
    IO = sb("IO", [128, 512])
    RSTD1 = sb("RSTD1", [128, 512])
    RSTD3 = RSTD1
    NSQ = 4
    SQ = [sb("SQ%d" % i, [128, 512], BF16) for i in range(NSQ)]
    TMP16 = sb("TMP16", [128, NS])
    NCAR = 4
    CAR = [dict(N0=sb("cN0_%d" % i, [128, 512]), N1=sb("cN1_%d" % i, [128, 512]),
                D0=sb("cD0_%d" % i, [128, 512], BF16)) for i in range(NCAR)]
    PD1 = sb("PD1", [128, 512], BF16)
    rPD1 = Res()
    HB = [sb("HB%d" % i, [128, BUFW]) for i in range(2)]
    HPB = [sb("HPB%d" % i, [128, BUFW]) for i in range(3)]
    HD = [[sb("HD%d_%d" % (i, j), [128, 512]) for j in range(3)] for i in range(2)]
    PS = [nc.alloc_psum_tensor("ps%d" % i, [128, 512], F32).ap() for i in range(8)]

    rX = [[Res() for _ in range(KD)] for _ in range(NT)]
    rXN = [[Res() for _ in range(KD)] for _ in range(2)]
    rMIX = [Res() for _ in range(16)]
    rWOP = [Res() for _ in range(NWO)]
    rWIN = [[Res(), Res()] for _ in range(NWS)]
    rSW = [[Res(), Res()] for _ in range(NWS)]
    rVT, rDER, rIDENT, rONES, rRC, rTMP16 = Res(), Res(), Res(), Res(), Res(), Res()
    rHS = [Res() for _ in range(KD)]
    rSSP = [Res() for _ in range(KD)]
    rSSC = [Res() for _ in range(KD)]
    rSSH = [Res() for _ in range(KD)]
    rIO = Res()
    rRSTD1 = Res()
    rRSTD3 = rRSTD1
    rSQ = [Res() for _ in range(NSQ)]
    rCAR = [dict(N0=Res(), N1=Res(), D0=Res()) for _ in range(NCAR)]
    rHB = [Res() for _ in range(2)]
    rHPB = [Res() for _ in range(3)]
    rHD = [[Res() for _ in range(3)] for _ in range(2)]
    rPS = [Res() for _ in range(8)]
    NOB = 6
    OBv = [HD[dc % 2][dc // 2] for dc in range(NOB)]
    rOB = [rHD[dc % 2][dc // 2] for dc in range(NOB)]
    ps_next = [0]

    ps_live = [False] * 8

    def ps_alloc():
        for _ in range(8):
            i = ps_next[0]
            ps_next[0] = (i + 1) % 8
            if not ps_live[i]:
                ps_live[i] = True
                return PS[i], rPS[i]
        raise RuntimeError("PSUM banks exhausted")

    def rel(rps):
        ps_live[rPS.index(rps)] = False

    NWARM = 0

    def warm(n, xb):
        if NWARM <= 0:
            return
        ps, rps = ps_alloc()

        def emit(e):
            ins = None
            for k in range(NWARM):
                ins = e.matmul(ps[:, 0:n], lhsT=ONES, rhs=XN[xb][:, k % KD, 0:n], start=(k == 0), stop=(k == NWARM - 1))
            return ins
        T.op("pe", emit, reads=[rONES] + rXN[xb], writes=[rps])
        rel(rps)

    io_pool = [(IO, rIO)]
    io_next = [0]

    def io_alloc():
        i = io_next[0] % len(io_pool)
        io_next[0] += 1
        return io_pool[i]

    def col(base, l, k):
        c = base + l * 8 + k
        return VT[:, c:c + 1]

    def dcol(j, l, k):
        return DER[:, j, l * 8 + k:l * 8 + k + 1]

    def as_t(ap2d, tl):
        if tl["kind"] == "p":
            return ap2d
        return ap2d.rearrange("p (s r) -> p s r", s=NS)

    def tview(buf, tl, H):
        if tl["kind"] == "p":
            return buf[:, 0:H + 512]
        return buf[:, 0:NS * (H + TS)].rearrange("p (s r) -> p s r", s=NS)

    def tsl(view, tl, lo, hi):
        if tl["kind"] == "p":
            return view[:, lo:hi]
        return view[:, :, lo:hi]

    for _i in range(2):
        for _j in range(3):
            io_pool.append((HD[_i][_j], rHD[_i][_j]))
    T.dma("sp", IDENT, ident_d, writes=[rIDENT])
    T.op("pool", lambda e: e.memset(ONES, 1.0), writes=[rONES])
    T.op("pool", lambda e: e.memset(RC, 1.0), writes=[rRC])
    for g, w in enumerate(WINS):
        for t in range(w - 1):
            T.op("pool", lambda e, g=g, t=t: e.memset(RC[:, g, t:t + 1], 1.0 / (t + 1)), writes=[rRC])
    r0 = 0
    while r0 < NVEC:
        nr = min(128, NVEC - r0)
        io, rio = io_alloc()
        T.dma("sp", io[0:nr, 0:128], vecs_d[r0:r0 + nr, :], writes=[rio])
        ps, rps = ps_alloc()
        T.op("pe", lambda e, io=io, nr=nr, ps=ps: e.transpose(ps[:, 0:nr], io[0:nr, 0:128], IDENT[0:nr, 0:nr]),
             reads=[rio, rIDENT], writes=[rps])
        T.op("act", lambda e, ps=ps, r0=r0, nr=nr: e.copy(VT[:, r0:r0 + nr], ps[:, 0:nr]), reads=[rps], writes=[rVT])
        rel(rps)
        r0 += nr
    T.op("dve", lambda e: e.tensor_scalar(DER[:, 0, :], VT[:, V_BA:V_BA + 32], 0.5, None, op0=ALU.mult), reads=[rVT], writes=[rDER])
    T.op("dve", lambda e: e.tensor_scalar(DER[:, 1, :], VT[:, V_BX:V_BX + 32], 0.5, None, op0=ALU.mult), reads=[rVT], writes=[rDER])
    T.op("act", lambda e: e.activation(DER[:, 4, :], VT[:, V_LAM:V_LAM + 32], AF.Exp, scale=-1.0), reads=[rVT], writes=[rDER])
    T.op("act", lambda e: e.activation(DER[:, 4, :], DER[:, 4, :], AF.Ln, bias=1.0, scale=1.0), reads=[rDER], writes=[rDER])
    T.op("dve", lambda e: e.tensor_scalar(DER[:, 2, :], DER[:, 4, :], -8.0, None, op0=ALU.mult), reads=[rDER], writes=[rDER])
    T.op("dve", lambda e: e.tensor_scalar(DER[:, 3, :], DER[:, 4, :], -4.0, None, op0=ALU.mult), reads=[rDER], writes=[rDER])

    bg = []
    bg_on = [False]

    def tr_in(src, nr, dst_fn, rdst):
        for hh in range(2):
            cell = {}

            def t_dma(hh=hh, cell=cell):
                io, rio = io_alloc()
                cell["io"] = (io, rio)
                T.dma("sp", io[0:nr, :], src[:, hh * 512:(hh + 1) * 512], writes=[rio])

            def t_tr(hh=hh, cell=cell):
                io, rio = cell["io"]
                ps, rps = ps_alloc()

                def emit(e):
                    ins = None
                    for j in range(4):
                        ins = e.transpose(ps[:, j * nr:(j + 1) * nr], io[0:nr, j * 128:(j + 1) * 128], IDENT[0:nr, 0:nr])
                    return ins
                T.op("pe", emit, reads=[rio, rIDENT], writes=[rps])
                T.op("act", lambda e: e.copy(dst_fn(hh), ps[:, 0:4 * nr].rearrange("p (j c) -> p j c", j=4)),
                     reads=[rps], writes=[rdst[hh * 4 + j] for j in range(4)])
                rel(rps)
            if bg_on[0]:
                bg.append(t_dma)
                bg.append(t_tr)
            else:
                t_dma()
                t_tr()

    def load_tokens(src_rows, tl_idx, c0):
        tr_in(src_rows, 128, lambda hh: X[:, hh * 4:hh * 4 + 4, c0:c0 + 128], rX[tl_idx])

    for tl in TILES:
        if tl["kind"] == "p":
            for b in range(4):
                load_tokens(xp_d[tl["t0"] + b * 128:tl["t0"] + (b + 1) * 128, :], tl["i"], tl["t0"] + b * 128)
        else:
            load_tokens(xs_d, tl["i"], tl["t0"])

    def load_sample_state(l):
        for hf in range(2):
            nr = 8 * HP
            tr_in(spool_d[l, hf * nr:(hf + 1) * nr, :], nr,
                  lambda hh, hf=hf: SSP[:, hh * 4:hh * 4 + 4, hf * 8:(hf + 1) * 8, :].rearrange("p k s r -> p k (s r)"), rSSP)
        tr_in(sconv_d[l, :, :], NS * HC, lambda hh: SSC[:, hh * 4:hh * 4 + 4, :, :].rearrange("p k s r -> p k (s r)"), rSSC)
        tr_in(slru_d[l, :, :], NS, lambda hh: SSH[:, hh * 4:hh * 4 + 4, :], rSSH)

    def store_rows(src_view_fn, nr, rsrc, dsts):
        for hh in range(2):
            def t_store(hh=hh):
                io, rio = io_alloc()
                ps, rps = ps_alloc()

                def emit(e):
                    ins = None
                    for j in range(4):
                        k = hh * 4 + j
                        ins = e.transpose(ps[0:nr, j * 128:(j + 1) * 128], src_view_fn(k), IDENT)
                    return ins
                T.op("pe", emit, reads=[rsrc[hh * 4 + j] for j in range(4)] + [rIDENT], writes=[rps])
                T.op("act", lambda e: e.copy(io[0:nr, :], ps[0:nr, :]), reads=[rps], writes=[rio])
                rel(rps)
                for (a, b, dst) in dsts:
                    T.dma("sp", dst[:, hh * 512:(hh + 1) * 512], io[a:b, :], reads=[rio])
            if bg_on[0]:
                bg.append(t_store)
            else:
                t_store()

    def bg_step(nmax=1):
        for _ in range(nmax):
            if bg:
                bg.pop(0)()

    def ones_mm(ps, rps, k, n, sq, rsq):
        T.op("pe", lambda e: e.matmul(ps[:, 0:n], lhsT=ONES, rhs=sq[:, 0:n], start=(k == 0), stop=(k == KD - 1)),
             reads=[rsq, rONES], writes=([rps] if k in (0, KD - 1) else []))

    def rstd_from(ps, rps, n, RS, rRS):
        T.op("act", lambda e: e.activation(RS[:, 0:n], ps[:, 0:n], AF.Ln, bias=EPS, scale=1.0 / D), reads=[rps], writes=[rRS])
        rel(rps)
        T.op("act", lambda e: e.activation(RS[:, 0:n], RS[:, 0:n], AF.Exp, scale=-0.5), reads=[rRS], writes=[rRS])

    sq_next = [0]

    def sq_alloc():
        i = sq_next[0]
        sq_next[0] = (i + 1) % NSQ
        return SQ[i], rSQ[i]

    p1 = {}

    def phase1_sq(l, tl, ks):
        n, t0, ti = tl["n"], tl["t0"], tl["i"]
        for k in ks:
            sq, rsq = sq_alloc()
            T.op("act", lambda e, k=k, sq=sq: e.activation(sq[:, 0:n], X[:, k, t0:t0 + n], AF.Square), reads=[rX[ti][k]], writes=[rsq])
            p1[k] = (sq, rsq)

    def phase1_mm(l, tl, ks):
        n = tl["n"]
        if 0 in ks:
            p1["ps"] = ps_alloc()
        ps, rps = p1["ps"]
        for k in ks:
            sq, rsq = p1[k]
            ones_mm(ps, rps, k, n, sq, rsq)

    def phase1_fin(l, tl, xb):
        n, t0, ti = tl["n"], tl["t0"], tl["i"]
        ps, rps = p1["ps"]
        rstd_from(ps, rps, n, RSTD1, rRSTD1)
        for k in range(KD):
            T.op("dve", lambda e, k=k: e.scalar_tensor_tensor(XN[xb][:, k, 0:n], X[:, k, t0:t0 + n], col(V_NPRE, l, k),
                                                              RSTD1[:, 0:n], op0=ALU.mult, op1=ALU.mult),
                 reads=[rX[ti][k], rRSTD1, rVT], writes=[rXN[xb][k]])

    def phase1(l, tl, xb):
        phase1_sq(l, tl, range(0, 4))
        phase1_mm(l, tl, range(0, 4))
        phase1_sq(l, tl, range(4, 8))
        phase1_mm(l, tl, range(4, 8))
        phase1_fin(l, tl, xb)

    def zmm(wbuf, rw, c, n, xb):
        ps, rps = ps_alloc()

        def emit(e, ps=ps):
            ins = None
            for k in range(KD):
                ins = e.matmul(ps[:, 0:n], lhsT=wbuf[:, k, c * 128:(c + 1) * 128], rhs=XN[xb][:, k, 0:n],
                               start=(k == 0), stop=(k == KD - 1))
            return ins
        T.op("pe", emit, reads=[rw[c // 2]] + rXN[xb], writes=[rps])
        return ps, rps

    def pool_item(l, tl, g, si, ws, xb):
        n, nt = tl["n"], tl["nt"]
        prompt = tl["kind"] == "p"
        w = WINS[g]
        m = w.bit_length() - 1
        car, rcar = CAR[si], rCAR[si]
        B = dict(H0=HPB[0], H1=HPB[1], H2=HPB[2], N0=car["N0"], N1=car["N1"])
        rB = dict(H0=rHPB[0], H1=rHPB[1], H2=rHPB[2], N0=rcar["N0"], N1=rcar["N1"])
        dbufs, rdb = [car["D0"], PD1], [rcar["D0"], rPD1]
        wbuf, rw, sw, rsw = WIN[ws], rWIN[ws], SW[ws], rSW[ws]
        st = {}

        def A():
            st["z"] = [zmm(wbuf, rw, c, n, xb) for c in range(4)]

        def chain(c):
            pc = g * 2 + c
            ps, rps = st["z"][c]
            AV = tview(B["H0"], tl, HP)
            if prompt:
                T.op("act", lambda e: e.copy(AV[:, 0:HP], HS[:, pc, 0:HP]), reads=[rHS[pc]], writes=[rB["H0"]])
            else:
                T.op("act", lambda e: e.copy(AV[:, :, 0:HP], SSP[:, pc, :, :]), reads=[rSSP[pc]], writes=[rB["H0"]])
            T.op("act", lambda e: e.copy(tsl(AV, tl, HP, HP + nt), as_t(ps[:, 0:n], tl)), reads=[rps], writes=[rB["H0"]])
            rel(rps)
            if prompt:
                T.op("pool", lambda e: e.tensor_copy(HS[:, pc, 0:HP], AV[:, nt:nt + HP]), reads=[rB["H0"]], writes=[rHS[pc]])
            else:
                T.op("pool", lambda e: e.tensor_copy(SSP[:, pc, :, :], AV[:, :, nt:nt + HP]), reads=[rB["H0"]], writes=[rSSP[pc]])
            lo = [0] * (m + 1)
            lo[m] = HP
            for j in range(m, 0, -1):
                lo[j - 1] = lo[j] - (1 << (j - 1))
            src_n = "H0"
            names = ["H1", "H2"]
            for j in range(1, m + 1):
                dst_n = names[(j - 1) % 2]
                sh = 1 << (j - 1)
                SV = tview(B[src_n], tl, HP)
                DV = tview(B[dst_n], tl, HP)
                T.op("dve", lambda e, SV=SV, DV=DV, j=j, sh=sh: e.tensor_tensor(
                    tsl(DV, tl, lo[j], HP + nt), tsl(SV, tl, lo[j], HP + nt), tsl(SV, tl, lo[j] - sh, HP + nt - sh), ALU.add),
                    reads=[rB[src_n]], writes=[rB[dst_n]])
                src_n = dst_n
            SV = tview(B[src_n], tl, HP)
            T.op("dve", lambda e: e.scalar_tensor_tensor(
                as_t(dbufs[c][:, 0:n], tl), tsl(SV, tl, HP, HP + nt), 1.0 / w, tsl(AV, tl, HP, HP + nt), op0=ALU.mult, op1=ALU.subtract),
                reads=[rB[src_n], rB["H0"]], writes=[rdb[c]])
            if tl["first"]:
                tmp_n = names[m % 2]
                TV = B[tmp_n]
                T.op("dve", lambda e: e.tensor_tensor(TV[:, 0:w - 1], SV[:, HP:HP + w - 1], RC[:, g, 0:w - 1], ALU.mult),
                     reads=[rB[src_n], rRC], writes=[rB[tmp_n]])
                T.op("dve", lambda e: e.tensor_tensor(dbufs[c][:, 0:w - 1], TV[:, 0:w - 1], AV[:, HP:HP + w - 1], ALU.subtract),
                     reads=[rB[tmp_n], rB["H0"]], writes=[rdb[c]])

        def Bst():
            chain(0)
            for mo in range(2):
                psg, rpsg = st["z"][2 + mo]
                sg_n = ("N0", "N1")[mo]
                T.op("act", lambda e, psg=psg, sg_n=sg_n: e.activation(B[sg_n][:, 0:n], psg[:, 0:n], AF.Silu), reads=[rpsg], writes=[rB[sg_n]])
                rel(rpsg)
            chain(1)

        def C():
            warm(n, xb)
            st["y"] = []
            for mo in range(2):
                psy, rpsy = ps_alloc()

                def emit(e, psy=psy, mo=mo):
                    ins = None
                    for k in range(2):
                        ins = e.matmul(psy[:, 0:n], lhsT=sw[:, k * 256 + mo * 128:k * 256 + (mo + 1) * 128], rhs=dbufs[k][:, 0:n],
                                       start=(k == 0), stop=(k == 1))
                    return ins
                T.op("pe", emit, reads=rsw + [rdb[0], rdb[1]], writes=[rpsy])
                st["y"].append((psy, rpsy))

        def Dst():
            for mo in range(2):
                pc = g * 2 + mo
                psy, rpsy = st["y"][mo]
                sg_n = ("N0", "N1")[mo]
                T.op("dve", lambda e, psy=psy, sg_n=sg_n, pc=pc: e.scalar_tensor_tensor(
                    MIX[:, pc, 0:n], psy[:, 0:n], col(V_PSC, l, pc), B[sg_n][:, 0:n], op0=ALU.mult, op1=ALU.mult),
                    reads=[rpsy, rB[sg_n], rVT], writes=[rMIX[pc]])
                rel(rpsy)

        return dict(A=A, B=Bst, C=C, D=Dst)

    def lru_item(l, tl, lc, si, ws, xb):
        n, nt = tl["n"], tl["nt"]
        prompt = tl["kind"] == "p"
        c = lc % 2
        car, rcar = CAR[si], rCAR[si]
        hb, hd = lc % 2, lc % 2
        B = dict(U=HB[hb], XA=car["N0"], SG=car["N1"], TA=HD[hd][0], TX=HD[hd][1], AA=HD[hd][2])
        rB = dict(U=rHB[hb], XA=rcar["N0"], SG=rcar["N1"], TA=rHD[hd][0], TX=rHD[hd][1], AA=rHD[hd][2])
        xsb, rxsb = car["D0"], rcar["D0"]
        wbuf, rw, sw, rsw = WIN[ws], rWIN[ws], SW[ws], rSW[ws]
        st = {}
        U, XA, TA, TX, SG, AA = "U", "XA", "TA", "TX", "SG", "AA"

        def A():
            st["u"] = zmm(wbuf, rw, c, n, xb)
            st["g"] = zmm(wbuf, rw, 2 + c, n, xb)

        def Bst():
            ps, rps = st["u"]
            UV = tview(B[U], tl, HC)
            if prompt:
                T.op("act", lambda e: e.copy(UV[:, 0:HC], HS[:, lc, 15:18]), reads=[rHS[lc]], writes=[rB[U]])
            else:
                T.op("act", lambda e: e.copy(UV[:, :, 0:HC], SSC[:, lc, :, :]), reads=[rSSC[lc]], writes=[rB[U]])
            T.op("act", lambda e: e.copy(tsl(UV, tl, HC, HC + nt), as_t(ps[:, 0:n], tl)), reads=[rps], writes=[rB[U]])
            rel(rps)
            if prompt:
                T.op("pool", lambda e: e.tensor_copy(HS[:, lc, 15:18], UV[:, nt:nt + HC]), reads=[rB[U]], writes=[rHS[lc]])
            else:
                T.op("pool", lambda e: e.tensor_copy(SSC[:, lc, :, :], UV[:, :, nt:nt + HC]), reads=[rB[U]], writes=[rSSC[lc]])
            XV = as_t(B[XA][:, 0:n], tl)
            cw0 = V_CW + l * 32 + lc
            T.op("dve", lambda e: e.tensor_scalar(XV, tsl(UV, tl, 0, nt), VT[:, cw0:cw0 + 1], col(V_CB, l, lc), op0=ALU.mult, op1=ALU.add),
                 reads=[rB[U], rVT], writes=[rB[XA]])
            for tap in range(1, 4):
                cwc = V_CW + l * 32 + tap * 8 + lc
                T.op("dve", lambda e, tap=tap, cwc=cwc: e.scalar_tensor_tensor(
                    XV, tsl(UV, tl, tap, tap + nt), VT[:, cwc:cwc + 1], XV, op0=ALU.mult, op1=ALU.add),
                    reads=[rB[U], rB[XA], rVT], writes=[rB[XA]])
            T.op("dve", lambda e: e.tensor_copy(xsb[:, 0:n], B[XA][:, 0:n]), reads=[rB[XA]], writes=[rxsb])
            psg, rpsg = st["g"]
            T.op("act", lambda e: e.activation(B[SG][:, 0:n], psg[:, 0:n], AF.Silu), reads=[rpsg], writes=[rB[SG]])
            rel(rpsg)

        def C():
            warm(n, xb)
            psa, rpsa = ps_alloc()
            T.op("pe", lambda e: e.matmul(psa[:, 0:n], lhsT=sw[:, c * 128:(c + 1) * 128], rhs=xsb[:, 0:n], start=True, stop=True),
                 reads=rsw + [rxsb], writes=[rpsa])
            psx, rpsx = ps_alloc()
            T.op("pe", lambda e: e.matmul(psx[:, 0:n], lhsT=sw[:, 256 + c * 128:256 + (c + 1) * 128], rhs=xsb[:, 0:n], start=True, stop=True),
                 reads=rsw + [rxsb], writes=[rpsx])
            st["a"] = (psa, rpsa)
            st["x"] = (psx, rpsx)

        def Dst():
            psa, rpsa = st["a"]
            psx, rpsx = st["x"]
            T.op("act", lambda e: e.activation(B[TA][:, 0:n], psa[:, 0:n], AF.Tanh, bias=dcol(0, l, lc), scale=0.5),
                 reads=[rpsa, rDER], writes=[rB[TA]])
            rel(rpsa)
            T.op("act", lambda e: e.activation(B[TX][:, 0:n], psx[:, 0:n], AF.Tanh, bias=dcol(1, l, lc), scale=0.5),
                 reads=[rpsx, rDER], writes=[rB[TX]])
            rel(rpsx)

        def D2():
            T.op("act", lambda e: e.activation(B[AA][:, 0:n], B[TA][:, 0:n], AF.Exp, bias=dcol(3, l, lc), scale=dcol(3, l, lc)),
                 reads=[rB[TA], rDER], writes=[rB[AA]])
            T.op("act", lambda e: e.activation(B[TA][:, 0:n], B[TA][:, 0:n], AF.Exp, bias=dcol(2, l, lc), scale=dcol(2, l, lc)),
                 reads=[rB[TA], rDER], writes=[rB[TA]])
            T.op("act", lambda e: e.activation(B[TA][:, 0:n], B[TA][:, 0:n], AF.Ln, bias=1.0, scale=-1.0), reads=[rB[TA]], writes=[rB[TA]])
            T.op("act", lambda e: e.activation(B[TA][:, 0:n], B[TA][:, 0:n], AF.Exp, scale=0.5), reads=[rB[TA]], writes=[rB[TA]])
            if tl["first"]:
                T.op("dve", lambda e: e.memset(B[TA][:, 0:1], 1.0), writes=[rB[TA]])
            T.op("dve", lambda e: e.scalar_tensor_tensor(B[TX][:, 0:n], B[TX][:, 0:n], 1.0, B[XA][:, 0:n], op0=ALU.add, op1=ALU.mult),
                 reads=[rB[TX], rB[XA]], writes=[rB[TX]])
            T.op("dve", lambda e: e.scalar_tensor_tensor(B[TX][:, 0:n], B[TX][:, 0:n], 0.5, B[TA][:, 0:n], op0=ALU.mult, op1=ALU.mult),
                 reads=[rB[TX], rB[TA]], writes=[rB[TX]])
            if prompt:
                T.op("dve", lambda e: e.tensor_tensor_scan(B[XA][:, 0:n], B[AA][:, 0:n], B[TX][:, 0:n], HS[:, lc, 18:19],
                                                           op0=ALU.mult, op1=ALU.add),
                     reads=[rB[AA], rB[TX], rHS[lc]], writes=[rB[XA]])
                T.op("dve", lambda e: e.tensor_copy(HS[:, lc, 18:19], B[XA][:, n - 1:n]), reads=[rB[XA]], writes=[rHS[lc]])
            else:
                A3 = as_t(B[AA][:, 0:n], tl)
                U3 = as_t(B[TX][:, 0:n], tl)
                H3 = as_t(B[XA][:, 0:n], tl)
                T.op("dve", lambda e: e.tensor_tensor(TMP16, A3[:, :, 0], SSH[:, lc, :], ALU.mult),
                     reads=[rB[AA], rSSH[lc]], writes=[rTMP16])
                T.op("dve", lambda e: e.tensor_tensor(U3[:, :, 0], U3[:, :, 0], TMP16, ALU.add),
                     reads=[rB[TX], rTMP16], writes=[rB[TX]])
                T.op("dve", lambda e: e.memset(A3[:, :, 0], 0.0), writes=[rB[AA]])
                T.op("dve", lambda e: e.tensor_tensor_scan(B[XA][:, 0:n], B[AA][:, 0:n], B[TX][:, 0:n], 0.0, op0=ALU.mult, op1=ALU.add),
                     reads=[rB[AA], rB[TX]], writes=[rB[XA]])
                T.op("dve", lambda e: e.tensor_copy(SSH[:, lc, :], H3[:, :, TS - 1]), reads=[rB[XA]], writes=[rSSH[lc]])
            T.op("dve", lambda e: e.tensor_tensor(MIX[:, 8 + lc, 0:n], B[XA][:, 0:n], B[SG][:, 0:n], ALU.mult),
                 reads=[rB[XA], rB[SG]], writes=[rMIX[8 + lc]])

        return dict(A=A, B=Bst, C=C, D1=Dst, D2=D2, lc=lc)

    def phase3(l, tl, last_layer, sidx3):
        n, t0, ti = tl["n"], tl["t0"], tl["i"]
        psn, rpsn = None, None
        evs = []
        held = {}
        for dc in range(KD):
            ps, rps = ps_alloc()

            pidx = sidx3 * KD + dc
            wslot = pidx % NWO

            def emit(e, ps=ps, dc=dc, wslot=wslot):
                ins = None
                for k in range(16):
                    ins = e.matmul(ps[:, 0:n], lhsT=WOP[wslot][:, k, :], rhs=MIX[:, k, 0:n], start=(k == 0), stop=(k == 15))
                return ins
            T.op("pe", emit, reads=[rWOP[wslot]] + rMIX, writes=[rps])
            issue_wout_piece(pidx + NWO)
            if dc < NOB:
                T.op("act", lambda e, ps=ps, dc=dc: e.copy(OBv[dc][:, 0:n], ps[:, 0:n]), reads=[rps], writes=[rOB[dc]])
            else:
                held[dc] = (ps, rps)
            sq, rsq = sq_alloc()
            T.op("act", lambda e, ps=ps, sq=sq: e.activation(sq[:, 0:n], ps[:, 0:n], AF.Square), reads=[rps], writes=[rsq])
            if dc < NOB:
                rel(rps)
            evs.append((dc, sq, rsq))
            if len(evs) >= 3:
                d0, s0, r0_ = evs[-3]
                if psn is None:
                    psn, rpsn = ps_alloc()
                ones_mm(psn, rpsn, d0, n, s0, r0_)
        for d0, s0, r0_ in evs[-2:]:
            ones_mm(psn, rpsn, d0, n, s0, r0_)
        rstd_from(psn, rpsn, n, RSTD3, rRSTD3)
        for dc in range(KD):
            if dc < NOB:
                src, rsrc, dst, rdst = OBv[dc], rOB[dc], OBv[dc], rOB[dc]
            else:
                (src, rsrc), dst, rdst = held[dc], OBv[dc - NOB], rOB[dc - NOB]
            T.op("dve", lambda e, dc=dc, src=src, dst=dst: e.scalar_tensor_tensor(dst[:, 0:n], src[:, 0:n], col(V_NPOST, l, dc), RSTD3[:, 0:n],
                                                                                  op0=ALU.mult, op1=ALU.mult),
                 reads=[rsrc, rRSTD3, rVT], writes=[rdst])
            if dc >= NOB:
                rel(rsrc)
            T.op("dve", lambda e, dc=dc, dst=dst: e.tensor_tensor(X[:, dc, t0:t0 + n], X[:, dc, t0:t0 + n], dst[:, 0:n], ALU.add),
                 reads=[rX[ti][dc], rdst], writes=[rX[ti][dc]])
        if last_layer:
            for b in range(n // 128):
                c0 = t0 + b * 128
                dst = yp_d[c0:c0 + 128, :] if tl["kind"] == "p" else ys_d
                store_rows(lambda k, c0=c0: X[:, k, c0:c0 + 128], 128, rX[ti], [(0, 128, dst)])

    def win_cols(gi):
        if gi < 4:
            return gi * 256, D + gi * 256
        j = gi - 4
        return 2 * D + j * 256, 3 * D + j * 256

    steps = [(l, tl) for l in range(n_layers) for tl in TILES]
    NG = len(steps) * 8

    def issue_weights(gidx):
        if gidx >= NG:
            return
        l = steps[gidx // 8][0]
        q, odd = (gidx % 8) // 2, gidx % 2
        gi = q if odd else 4 + q
        s = gidx % NWS
        T.dma("pool", WIN[s].rearrange("p k e -> p (k e)"), w_in_d[l, gi], writes=[rWIN[s][0], rWIN[s][1]])
        if gi < 4:
            T.dma("pool", SW[s].rearrange("p (k d) -> p k d", k=2), pool_w_d[l, gi].rearrange("(k p) d -> p k d", p=128), writes=rSW[s])
        else:
            j = gi - 4
            T.dma("pool", SW[s][:, 0:256].rearrange("p (h j) -> p h j", h=2), lru_wa_d[l, 2 * j:2 * j + 2].rearrange("h i j -> i h j"), writes=[rSW[s][0]])
            T.dma("pool", SW[s][:, 256:512].rearrange("p (h j) -> p h j", h=2), lru_wx_d[l, 2 * j:2 * j + 2].rearrange("h i j -> i h j"), writes=[rSW[s][1]])

    def issue_wout_piece(pidx):
        if pidx >= len(steps) * KD:
            return
        l = steps[pidx // KD][0]
        dc = pidx % KD
        T.dma("pool", WOP[pidx % NWO].rearrange("p k d -> p (k d)"), w_out_d[l, dc], writes=[rWOP[pidx % NWO]])

    items = []
    icount = 0
    for sidx, (l, tl) in enumerate(steps):
        xb = sidx % 2
        lst = []
        for q in range(4):
            lst.append(("lru", 2 * q, sidx * 8 + 2 * q))
            lst.append(("lru", 2 * q + 1, sidx * 8 + 2 * q))
            lst.append(("pool", q, sidx * 8 + 2 * q + 1))
        for j, (kind, idx, gidx) in enumerate(lst):
            si = icount % NCAR
            icount += 1
            ws = gidx % NWS
            it = pool_item(l, tl, idx, si, ws, xb) if kind == "pool" else lru_item(l, tl, idx, si, ws, xb)
            it.update(sidx=sidx, j=j, l=l, tl=tl, gidx=gidx, gstart=(kind == "pool" or idx % 2 == 0),
                      glast=(kind == "pool" or idx % 2 == 1), last=(j == len(lst) - 1))
            items.append(it)

    def end_of_step(sidx):
        l, tl = steps[sidx]
        phase3(l, tl, l == n_layers - 1, sidx)
        if tl["last"]:
            store_rows(lambda k: HS[:, k, :], 19, rHS, [(0, 15, npp_d[l]), (15, 18, ncp_d[l]), (18, 19, nlp_d[l])])
        if tl["kind"] == "s":
            for hf in range(2):
                store_rows(lambda k, hf=hf: SSP[:, k, hf * 8:(hf + 1) * 8, :].rearrange("p s r -> p (s r)"), 8 * HP, rSSP,
                           [(0, 8 * HP, nps_d[l, hf * 8 * HP:(hf + 1) * 8 * HP, :])])
            store_rows(lambda k: SSC[:, k, :, :].rearrange("p s r -> p (s r)"), NS * HC, rSSC, [(0, NS * HC, ncs_d[l])])
            store_rows(lambda k: SSH[:, k, :], NS, rSSH, [(0, NS, nls_d[l])])
            if l + 1 < n_layers:
                load_sample_state(l + 1)

    del io_pool[1:]
    issue_weights(0)
    issue_weights(1)
    issue_weights(2)
    for _p in range(NWO):
        issue_wout_piece(_p)
    load_sample_state(0)
    phase1(steps[0][0], steps[0][1], 0)
    bg_on[0] = True
    hist = []
    pending = None

    deferred = []

    def retire_d(old):
        if "D" in old:
            while deferred:
                deferred.pop(0)["D2"]()
            old["D"]()
        elif old["lc"] % 2 == 0:
            old["D1"]()
            deferred.append(old)
        else:
            old["D1"]()
            while deferred:
                deferred.pop(0)["D2"]()
            old["D2"]()

    def retire(old):
        nonlocal pending
        old["C"]()
        if old["glast"]:
            issue_weights(old["gidx"] + NWS)
        retire_d(old)
        if old["last"]:
            pending = old["sidx"]

    for it in items:
        sidx, l, tl = it["sidx"], it["l"], it["tl"]
        if it["j"] == 0 and tl["i"] == 0:
            for k in range(KD):
                T.op("dve", lambda e, k=k: e.memset(HS[:, k, :], 0.0), writes=[rHS[k]])
        it["A"]()
        old = hist[-2] if len(hist) >= 2 else None
        if pending is not None:
            it["B"]()
            end_of_step(pending)
            pending = None
            if old is not None:
                old["C"]()
                if old["glast"]:
                    issue_weights(old["gidx"] + NWS)
        else:
            if old is not None:
                old["C"]()
                if old["glast"]:
                    issue_weights(old["gidx"] + NWS)
            it["B"]()
        if old is not None:
            retire_d(old)
            if old["last"]:
                pending = old["sidx"]
        bg_step(1)
        if sidx + 1 < len(steps):
            nl, ntl = steps[sidx + 1]
            if it["j"] == 4:
                phase1_sq(nl, ntl, range(0, 4))
            elif it["j"] == 5:
                phase1_mm(nl, ntl, range(0, 4))
                phase1_sq(nl, ntl, range(4, 8))
            elif it["j"] == 6:
                phase1_mm(nl, ntl, range(4, 8))
                phase1_fin(nl, ntl, (sidx + 1) % 2)
        hist.append(it)
    for old in hist[-2:]:
        if pending is not None:
            end_of_step(pending)
            pending = None
        retire(old)
    end_of_step(pending)
    for _i in range(2):
        for _j in range(3):
            io_pool.append((HD[_i][_j], rHD[_i][_j]))
    while bg:
        bg_step(1)

    T.finish("sp")
    print("[kernel] ops=%d waits=%d sbuf_left=%d" % (T.nops, T.nwaits, nc.sbuf_bytes_remaining), flush=True)
    return nc


def _host_vecs(inp):
    rows = []
    for nm in ("norm_pre", "norm_post", "pool_scale", "conv_b", "lru_ba", "lru_bx", "lru_lam"):
        rows.append(np.asarray(inp[nm], np.float32).reshape(DEPTH * 8, 128))
    rows.append(np.asarray(inp["conv_w"], np.float32).reshape(DEPTH * 4 * 8, 128))
    return np.ascontiguousarray(np.concatenate(rows, axis=0))


_NC_CACHE = {}


def kernel(**inp):
    n_layers = int(os.environ.get("MK_LAYERS", DEPTH))
    if n_layers not in _NC_CACHE:
        _NC_CACHE[n_layers] = build_program(n_layers)
    nc = _NC_CACHE[n_layers]
    f = lambda a: np.ascontiguousarray(np.asarray(a, np.float32))
    vecs = _host_vecs(inp)
    ident = np.eye(128, dtype=np.float32)
    w_in = f(inp["w_in"]); w_out = f(inp["w_out"])
    grp = []
    for gi in range(8):
        ca, cb = (gi * 256, D + gi * 256) if gi < 4 else (2 * D + (gi - 4) * 256, 3 * D + (gi - 4) * 256)
        g2 = np.concatenate([w_in[:, :, ca:ca + 256], w_in[:, :, cb:cb + 256]], axis=2)
        grp.append(g2.reshape(DEPTH, KD, 128, 512).transpose(0, 2, 1, 3).reshape(DEPTH, 128, KD * 512))
    w_in_r = np.ascontiguousarray(np.stack(grp, axis=1))
    w_out_r = np.ascontiguousarray(w_out.reshape(DEPTH, 16, 128, KD, 128).transpose(0, 3, 2, 1, 4).reshape(DEPTH, KD, 128, 16 * 128))
    shared = dict(vecs=vecs, ident=ident, w_in_r=w_in_r, pool_w=f(inp["pool_w"]), lru_wa=f(inp["lru_wa"]),
                  lru_wx=f(inp["lru_wx"]), w_out_r=w_out_r)
    xp = f(inp["x_prompt"]); xs = f(inp["x_sample"])
    sp = f(inp["state_pool"]); sc = f(inp["state_conv"]); sl = f(inp["state_lru"])
    in_maps = []
    for b in range(NCORES):
        q = slice(b * NS, (b + 1) * NS)
        m = dict(shared)
        m["xp"] = xp[b]
        m["xs"] = xs[q].reshape(NS * TS, D)
        m["spool"] = np.ascontiguousarray(sp[:, q].reshape(DEPTH, NS * HP, D))
        m["sconv"] = np.ascontiguousarray(sc[:, q].reshape(DEPTH, NS * HC, D))
        m["slru"] = np.ascontiguousarray(sl[:, q].reshape(DEPTH, NS, D))
        in_maps.append(m)
    res = run_bass_kernel_spmd(nc, in_maps, core_ids=list(range(NCORES)))
    R = res.results
    g = lambda nm: [np.asarray(R[b][nm], np.float32) for b in range(NCORES)]
    y_prompt = np.stack(g("yp"), axis=0)
    y_sample = np.concatenate([a.reshape(NS, TS, D) for a in g("ys")], axis=0)
    npp = np.stack(g("npp"), axis=1)
    ncp = np.stack(g("ncp"), axis=1)
    nlp = np.stack([a.reshape(DEPTH, D) for a in g("nlp")], axis=1)
    nps = np.concatenate([a.reshape(DEPTH, NS, HP, D) for a in g("nps")], axis=1)
    ncs = np.concatenate([a.reshape(DEPTH, NS, HC, D) for a in g("ncs")], axis=1)
    nls = np.concatenate([a.reshape(DEPTH, NS, D) for a in g("nls")], axis=1)
    return (y_prompt, y_sample, npp, ncp, nlp, nps, ncs, nls)
```
